# Optimizing a Trainium2 kernel written in Bass

```python
import jax, jax.numpy as jnp
from jax import lax
import numpy as np

D_MODEL = 1024
BATCH = 16
SEQ = 2048
DEPTH = 4

HEAD_DIM = 64
N_FOX_HEADS = 8
N_MOBA_HEADS = 8
FOX_WIDTH = N_FOX_HEADS * HEAD_DIM
MOBA_WIDTH = N_MOBA_HEADS * HEAD_DIM
ATTN_WIDTH = FOX_WIDTH + MOBA_WIDTH
FOX_Q_BLOCK = 128
MOBA_BLOCK = 256
MOBA_TOPK = 3
MOBA_Q_CHUNK = 16
POOL_EXPAND = 2
POOL_WIDTH = POOL_EXPAND * D_MODEL
POOL_WINDOWS = (2, 4, 8, 16)
POOL_GROUP = POOL_WIDTH // len(POOL_WINDOWS)
ATTN_SPLIT_SIZES = (FOX_WIDTH, FOX_WIDTH, FOX_WIDTH, FOX_WIDTH, N_FOX_HEADS,
                    MOBA_WIDTH, MOBA_WIDTH, MOBA_WIDTH, MOBA_WIDTH)
ATTN_IN = sum(ATTN_SPLIT_SIZES)
DEEPNORM_ALPHA = (2 * DEPTH) ** 0.25
DEEPNORM_BETA = (8 * DEPTH) ** -0.25
LN_EPS = 1e-5
N_ATTN_LAYERS = (DEPTH + 1) // 2
N_POOL_LAYERS = DEPTH // 2

kernel_name = "fox_moba_pool_deepnorm_hybrid"


def layer_norm(x, g, b):
    xf = x.astype(jnp.float32)
    mu = jnp.mean(xf, axis=-1, keepdims=True)
    var = jnp.mean(jnp.square(xf - mu), axis=-1, keepdims=True)
    y = (xf - mu) * lax.rsqrt(var + LN_EPS) * g.astype(jnp.float32) + b.astype(jnp.float32)
    return y.astype(x.dtype)


def split_heads(t, n_heads):
    B, S, _ = t.shape
    return t.reshape(B, S, n_heads, HEAD_DIM).transpose(0, 2, 1, 3)


def merge_heads(t):
    B, H, S, Dh = t.shape
    return t.transpose(0, 2, 1, 3).reshape(B, S, H * Dh)


def fox_attention(q, k, v, log_f):
    B, H, S, Dh = q.shape
    c = jnp.cumsum(log_f, axis=-1)
    nq = S // FOX_Q_BLOCK
    qb = q.reshape(B, H, nq, FOX_Q_BLOCK, Dh).transpose(2, 0, 1, 3, 4)
    cb = c.reshape(B, H, nq, FOX_Q_BLOCK).transpose(2, 0, 1, 3)
    key_pos = jnp.arange(S)
    scale = Dh ** -0.5

    def block(args):
        i, q_i, c_i = args
        s = jnp.einsum('bhqd,bhkd->bhqk', q_i, k, preferred_element_type=jnp.float32) * scale
        s = s + (c_i[..., :, None] - c[..., None, :])
        q_pos = i * FOX_Q_BLOCK + jnp.arange(FOX_Q_BLOCK)
        s = jnp.where(key_pos[None, :] <= q_pos[:, None], s, -jnp.inf)
        p = jax.nn.softmax(s, axis=-1)
        return jnp.einsum('bhqk,bhkd->bhqd', p.astype(v.dtype), v)

    out = lax.map(block, (jnp.arange(nq), qb, cb))
    return out.transpose(1, 2, 0, 3, 4).reshape(B, H, S, Dh)


def moba_attention(q, k, v):
    B, H, S, Dh = q.shape
    nb = -(-S // MOBA_BLOCK)
    pad = nb * MOBA_BLOCK - S
    kb = jnp.pad(k, ((0, 0), (0, 0), (0, pad), (0, 0))).reshape(B, H, nb, MOBA_BLOCK, Dh)
    vb = jnp.pad(v, ((0, 0), (0, 0), (0, pad), (0, 0))).reshape(B, H, nb, MOBA_BLOCK, Dh)
    k_mean = jnp.mean(kb.astype(jnp.float32), axis=3)
    gate = jnp.einsum('bhsd,bhnd->bhsn', q.astype(jnp.float32), k_mean)
    q_blk = jnp.arange(S) // MOBA_BLOCK
    fully_past = jnp.arange(nb)[None, :] < q_blk[:, None]
    gate = jnp.where(fully_past, gate, -jnp.inf)
    kk = min(MOBA_TOPK, nb)
    top_val, top_idx = lax.top_k(gate, kk)
    valid = jnp.isfinite(top_val)

    C = MOBA_Q_CHUNK
    nc = S // C
    qc = q.reshape(B, H, nc, C, Dh).transpose(2, 0, 1, 3, 4)
    idx_c = top_idx.reshape(B, H, nc, C, kk).transpose(2, 0, 1, 3, 4)
    val_c = valid.reshape(B, H, nc, C, kk).transpose(2, 0, 1, 3, 4)
    scale = Dh ** -0.5
    gather = jax.vmap(jax.vmap(lambda blocks, ids: blocks[ids]))

    def chunk(args):
        i, q_i, idx_i, valid_i = args
        start = i * C
        blk = start // MOBA_BLOCK
        k_own = lax.dynamic_index_in_dim(kb, blk, axis=2, keepdims=False)
        v_own = lax.dynamic_index_in_dim(vb, blk, axis=2, keepdims=False)
        s_own = jnp.einsum('bhqd,bhkd->bhqk', q_i, k_own, preferred_element_type=jnp.float32) * scale
        q_pos = start + jnp.arange(C)
        k_pos = blk * MOBA_BLOCK + jnp.arange(MOBA_BLOCK)
        s_own = jnp.where(k_pos[None, :] <= q_pos[:, None], s_own, -jnp.inf)
        k_sel = gather(kb, idx_i)
        v_sel = gather(vb, idx_i)
        s_sel = jnp.einsum('bhqd,bhqnkd->bhqnk', q_i, k_sel, preferred_element_type=jnp.float32) * scale
        s_sel = jnp.where(valid_i[..., None], s_sel, -jnp.inf).reshape(B, H, C, kk * MOBA_BLOCK)
        p = jax.nn.softmax(jnp.concatenate([s_sel, s_own], axis=-1), axis=-1).astype(v.dtype)
        p_sel = p[..., :kk * MOBA_BLOCK].reshape(B, H, C, kk, MOBA_BLOCK)
        p_own = p[..., kk * MOBA_BLOCK:]
        return (jnp.einsum('bhqnk,bhqnkd->bhqd', p_sel, v_sel)
                + jnp.einsum('bhqk,bhkd->bhqd', p_own, v_own))

    out = lax.map(chunk, (jnp.arange(nc), qc, idx_c, val_c))
    return out.transpose(1, 2, 0, 3, 4).reshape(B, H, S, Dh)


def causal_multiscale_pool(u):
    B, S, W = u.shape
    uf = u.astype(jnp.float32)
    cs = jnp.pad(jnp.cumsum(uf, axis=1), ((0, 0), (1, 0), (0, 0)))
    t = jnp.arange(S)
    outs = []
    for g, w in enumerate(POOL_WINDOWS):
        cs_g = cs[..., g * POOL_GROUP:(g + 1) * POOL_GROUP]
        lo = jnp.maximum(t + 1 - w, 0)
        cnt = jnp.minimum(t + 1, w).astype(jnp.float32)
        outs.append((cs_g[:, t + 1] - cs_g[:, lo]) / cnt[None, :, None])
    return (jnp.concatenate(outs, axis=-1) - uf).astype(u.dtype)


def attn_sublayer(x, w_in, b_f, w_out):
    h = x @ w_in
    offsets = [int(o) for o in np.cumsum(ATTN_SPLIT_SIZES)[:-1]]
    fq, fk, fv, fg, ff, mq, mk, mv, mg = jnp.split(h, offsets, axis=-1)
    log_f = jax.nn.log_sigmoid((ff + b_f).astype(jnp.float32)).transpose(0, 2, 1)
    y_fox = fox_attention(split_heads(fq, N_FOX_HEADS), split_heads(fk, N_FOX_HEADS),
                          split_heads(fv, N_FOX_HEADS), log_f)
    y_moba = moba_attention(split_heads(mq, N_MOBA_HEADS), split_heads(mk, N_MOBA_HEADS),
                            split_heads(mv, N_MOBA_HEADS))
    y = jnp.concatenate([merge_heads(y_fox) * jax.nn.silu(fg),
                         merge_heads(y_moba) * jax.nn.silu(mg)], axis=-1)
    return y @ w_out


def pool_sublayer(x, w_in, w_grp, scale, w_out):
    B, S, _ = x.shape
    h = x @ w_in
    u, gate = h[..., :POOL_WIDTH], h[..., POOL_WIDTH:]
    pooled = causal_multiscale_pool(u).reshape(B, S, len(POOL_WINDOWS), POOL_GROUP)
    y = jnp.einsum('bsgc,gcd->bsgd', pooled, w_grp).reshape(B, S, POOL_WIDTH) * scale
    return (y * jax.nn.silu(gate)) @ w_out


def setup_inputs(seed: int = 0) -> dict:
    key = jax.random.key(seed)
    ks = jax.random.split(key, 12)
    f32 = jnp.float32
    x = jax.random.normal(ks[0], (BATCH, SEQ, D_MODEL), f32)
    attn_w_in = jax.random.normal(ks[1], (N_ATTN_LAYERS, D_MODEL, ATTN_IN), f32) * D_MODEL ** -0.5
    attn_b_f = jax.random.uniform(ks[2], (N_ATTN_LAYERS, N_FOX_HEADS), f32, 1.0, 4.0)
    attn_w_out = (jax.random.normal(ks[3], (N_ATTN_LAYERS, ATTN_WIDTH, D_MODEL), f32)
                  * ATTN_WIDTH ** -0.5 * DEEPNORM_BETA)
    pool_w_in = jax.random.normal(ks[4], (N_POOL_LAYERS, D_MODEL, 2 * POOL_WIDTH), f32) * D_MODEL ** -0.5
    pool_w_grp = (jax.random.normal(ks[5], (N_POOL_LAYERS, len(POOL_WINDOWS), POOL_GROUP, POOL_GROUP), f32)
                  * POOL_GROUP ** -0.5)
    pool_scale = 1.0 + 0.1 * jax.random.normal(ks[6], (N_POOL_LAYERS, POOL_WIDTH), f32)
    pool_w_out = (jax.random.normal(ks[7], (N_POOL_LAYERS, POOL_WIDTH, D_MODEL), f32)
                  * POOL_WIDTH ** -0.5 * DEEPNORM_BETA)
    ln_g = 1.0 + 0.02 * jax.random.normal(ks[8], (DEPTH, D_MODEL), f32)
    ln_b = 0.02 * jax.random.normal(ks[9], (DEPTH, D_MODEL), f32)
    return {"x": x, "attn_w_in": attn_w_in, "attn_b_f": attn_b_f, "attn_w_out": attn_w_out,
            "pool_w_in": pool_w_in, "pool_w_grp": pool_w_grp, "pool_scale": pool_scale,
            "pool_w_out": pool_w_out, "ln_g": ln_g, "ln_b": ln_b}


def reference(x, attn_w_in, attn_b_f, attn_w_out, pool_w_in, pool_w_grp, pool_scale,
              pool_w_out, ln_g, ln_b):
    for layer in range(DEPTH):
        j = layer // 2
        if layer % 2 == 0:
            f = attn_sublayer(x, attn_w_in[j], attn_b_f[j], attn_w_out[j])
        else:
            f = pool_sublayer(x, pool_w_in[j], pool_w_grp[j], pool_scale[j], pool_w_out[j])
        x = layer_norm(DEEPNORM_ALPHA * x + f, ln_g[layer], ln_b[layer])
    return x
```

```python
from concourse.bass_utils import run_bass_kernel_spmd
import numpy as np
import concourse.bass as bass
import concourse.mybir as mybir

F32 = mybir.dt.float32
BF16 = mybir.dt.bfloat16
AF = mybir.ActivationFunctionType
ALU = mybir.AluOpType
AX = mybir.AxisListType

COMPUTE = ("pe", "act", "dve", "pool")
EPOCH = 12000


class T:
    _n = 0

    def __init__(self, P, name, shape, dtype, space="sbuf", parts=1):
        T._n += 1
        self.id = T._n
        self.name = name
        self.parts = parts
        self.space = space
        nc = P.nc
        nm = "%s_%d" % (name, self.id)
        if space == "sbuf":
            self.h = nc.alloc_sbuf_tensor(nm, list(shape), dtype)
        else:
            self.h = nc.alloc_psum_tensor(nm, list(shape), dtype)

    def __getitem__(self, idx):
        return self.h[idx]

    def alias(self):
        a = T.__new__(T)
        T._n += 1
        a.id, a.name, a.parts, a.space, a.h = T._n, self.name + "_al", 1, self.space, self.h
        return a

    def keys(self, parts=None):
        if parts is None:
            return [(self.id, p) for p in range(self.parts)]
        if isinstance(parts, int):
            return [(self.id, parts)]
        return [(self.id, p) for p in parts]


def _psum_keys(lst):
    out = []
    for x in lst:
        t = x[0] if isinstance(x, tuple) else x
        if isinstance(t, T) and t.space == "psum":
            out += t.keys()
    return out


def _keys(lst):
    out = []
    for x in lst:
        if x is None:
            continue
        if isinstance(x, T):
            out += x.keys()
        elif isinstance(x, tuple) and isinstance(x[0], T):
            out += x[0].keys(x[1])
        else:
            out.append(("d", x))
    return out


class Op:
    __slots__ = ("eng", "fn", "reads", "writes", "dma", "deps", "signal", "seq",
                 "sigval", "sem", "waits", "slot", "prevwait")


class Prog:
    def __init__(self, nc, n_dma_sems=24):
        self.nc = nc
        self.ops = []
        self.eng = {"pe": nc.tensor, "act": nc.scalar, "dve": nc.vector,
                    "pool": nc.gpsimd, "sp": nc.sync}
        self.n_dma_sems = n_dma_sems

    def tile(self, name, shape, dtype, space="sbuf", parts=1):
        return T(self, name, shape, dtype, space, parts)

    def op(self, eng, name, *args, reads=(), writes=(), **kw):
        o = Op()
        o.eng = eng
        o.fn = (lambda e, name=name, args=args, kw=kw: getattr(e, name)(*args, **kw))
        o.reads = _keys(reads)
        o.writes = _keys(writes) + _psum_keys(reads)
        o.dma = False
        self.ops.append(o)
        return o

    def dma(self, q, out, in_, reads=(), writes=()):
        o = Op()
        o.eng = q
        o.fn = (lambda e, out=out, in_=in_: e.dma_start(out=out, in_=in_))
        o.reads = _keys(reads)
        o.writes = _keys(writes)
        o.dma = True
        self.ops.append(o)
        return o

    def finalize(self):
        nc = self.nc
        ops = self.ops
        last_w = {}
        readers = {}
        for i, o in enumerate(ops):
            deps = set()
            for k in o.reads:
                j = last_w.get(k)
                if j is not None:
                    deps.add(j)
            for k in o.writes:
                j = last_w.get(k)
                if j is not None:
                    deps.add(j)
                for r in readers.get(k, ()):
                    deps.add(r)
            deps.discard(i)
            o.deps = sorted(deps)
            for k in o.reads:
                readers.setdefault(k, []).append(i)
            for k in o.writes:
                last_w[k] = i
                readers[k] = []
        seqc = {e: 0 for e in self.eng}
        known = {e: {} for e in self.eng}
        snap = [None] * len(ops)
        dma_slot_ctr = {e: 0 for e in self.eng}
        slot_last = {}
        for i, o in enumerate(ops):
            E = o.eng
            o.signal = False
            o.waits = []
            kn = known[E]
            o.seq = seqc[E]
            seqc[E] += 1
            for j in o.deps:
                p = ops[j]
                if p.dma:
                    key = ("d", j)
                    if kn.get(key):
                        continue
                    o.waits.append(j)
                    kn = dict(kn)
                    kn[key] = 1
                    for kk, vv in snap[j].items():
                        if kn.get(kk, -1) < vv:
                            kn[kk] = vv
                else:
                    B = p.eng
                    if B == E and E == "pe":
                        continue
                    key = ("c", B)
                    if kn.get(key, -1) >= p.seq:
                        continue
                    o.waits.append(j)
                    p.signal = True
                    kn = dict(kn)
                    kn[key] = p.seq
                    for kk, vv in snap[j].items():
                        if kn.get(kk, -1) < vv:
                            kn[kk] = vv
            o.prevwait = None
            if o.dma:
                s = dma_slot_ctr[E] % self.n_dma_sems
                dma_slot_ctr[E] += 1
                o.slot = s
                pj = slot_last.get((E, s))
                if pj is not None and not kn.get(("d", pj)):
                    o.prevwait = pj
                    kn = dict(kn)
                    kn[("d", pj)] = 1
                slot_last[(E, s)] = i
            known[E] = kn
            snap[i] = kn
        csem = {e: [] for e in COMPUTE}
        ccount = {e: 0 for e in COMPUTE}
        dsem = {}
        dcount = {}
        nw = 0
        for i, o in enumerate(ops):
            E = o.eng
            e = self.eng[E]
            wl = list(o.waits)
            if o.prevwait is not None:
                wl.append(o.prevwait)
            for j in wl:
                p = ops[j]
                e.wait_ge(p.sem, p.sigval)
                nw += 1
            ins = o.fn(e)
            if o.dma:
                key = (E, o.slot)
                if key not in dsem:
                    dsem[key] = nc.alloc_semaphore("d_%s_%d" % key)
                    dcount[key] = 0
                dcount[key] += 16
                o.sem = dsem[key]
                o.sigval = dcount[key]
                ins.then_inc(o.sem, 16)
            elif o.signal:
                if ccount[E] % EPOCH == 0:
                    csem[E].append(nc.alloc_semaphore("c_%s_%d" % (E, len(csem[E]))))
                ccount[E] += 1
                o.sem = csem[E][-1]
                o.sigval = (ccount[E] - 1) % EPOCH + 1
                ins.then_inc(o.sem, 1)
        sp = self.eng["sp"]
        for key, s in dsem.items():
            sp.wait_ge(s, dcount[key])
        self.stats = dict(n_ops=len(ops), n_waits=nw,
                          n_sig=sum(ccount.values()), per_eng=dict(seqc))
        return self.stats
S = 2048
D = 1024
NT = 16
ALPHA = (2 * 4) ** 0.25
LN_EPS = 1e-5
NEG = -30000.0
SLOT = 2176
NWS = 3
NAR = 17


class Ctx:
    pass


def build_program(layers=("A0", "P0", "A1", "P1"), nseq=2, dbg=None):
    nc = bass.Bass("TRN2", target_bir_lowering=False)
    P = Prog(nc)
    c = Ctx()
    c.nc, c.P = nc, P

    def din(name, shape):
        return nc.dram_tensor(name, list(shape), F32, kind="ExternalInput").ap()

    c.x = din("x", [2, S, D])
    c.aw_in = din("aw_in", [2, 8, 128, 8, 512])
    c.aw_ff = din("aw_ff", [2, 128, 8, 8])
    c.aw_out = din("aw_out", [2, 128, 8, 1024])
    c.ab_f = din("ab_f", [8, 2])
    c.pw_u = din("pw_u", [2, 4, 128, 8, 512])
    c.pw_g = din("pw_g", [2, 4, 128, 8, 512])
    c.pw_grp = din("pw_grp", [2, 4, 128, 4, 512])
    c.pw_out = din("pw_out", [2, 4, 128, 4, 1024])
    c.psc_d = din("psc", [128, 2, 16])
    c.lng_d = din("lng", [128, 4, 1024])
    c.lnb_d = din("lnb", [128, 4, 1024])
    c.cident_d = din("c_ident", [128, 128])
    c.ccausal_d = din("c_causal", [128, 128])
    c.cband_d = din("c_band", [128, 4, 144])
    c.cfirst_d = din("c_first", [128, 4, 2, 128])
    c.cblk_d = din("c_blk", [8, 2048])
    c.cpast_d = din("c_past", [128, 16, 8])
    c.cnotown_d = din("c_notown", [128, 16, 8])
    c.out = nc.dram_tensor("out", [2, S, D], F32, kind="ExternalOutput").ap()

    c.X = P.tile("X", [128, NT, D], F32, parts=NT)
    c.XT = P.tile("XT", [128, 8, S], BF16, parts=NT)
    c.WS = [P.tile("WS", [128, 4096], BF16) for _ in range(NWS)]
    c.WSH = P.tile("WSH", [128, 2048], BF16)
    c.AR = [P.tile("AR", [128, SLOT], BF16, parts=(4 if i == 13 else 1)) for i in range(NAR)]
    c.PS = [P.tile("PS", [128, 512], F32, space="psum") for _ in range(8)]
    c.ident = P.tile("ident", [128, 128], BF16)
    c.identF = P.tile("identF", [128, 128], F32)
    c.causal = P.tile("causal", [128, 128], BF16)
    c.band = P.tile("band", [128, 4, 144], BF16)
    c.first = P.tile("first", [128, 4, 2, 128], BF16)
    c.LNG = c.AR[12]
    c.LNB = c.AR[13]
    c.lng_v = c.LNG[:, 0:2048].bitcast(F32)
    c.lnb_v = c.LNB[:, 0:2048].bitcast(F32)
    c.psc = P.tile("psc", [128, 2, 16], F32)
    c.lst = P.tile("lst", [128, NT, 12], F32, parts=NT)
    c.lmv = P.tile("lmv", [128, NT, 4], F32, parts=NT)
    c.dbg = dbg
    c.ws_i = 0
    c.ps_i = 0
    c.sg_i = 0
    c.ev_i = 0

    P.dma("pool", c.ident[:], c.cident_d[:, :], writes=[c.ident])
    P.dma("sp", c.identF[:], c.cident_d[:, :], writes=[c.identF])
    P.dma("pool", c.causal[:], c.ccausal_d[:, :], writes=[c.causal])
    P.dma("pool", c.band[:], c.cband_d[:, :, :], writes=[c.band])
    P.dma("pool", c.first[:], c.cfirst_d[:, :, :, :], writes=[c.first])
    P.dma("sp", c.psc[:], c.psc_d[:, :, :], writes=[c.psc])
    attn_setup(c)

    for seq in range(nseq):
        xv = c.x[seq].rearrange("(t p) d -> p t d", p=128)
        for t in range(NT):
            P.dma("sp", c.X[:, t, :], xv[:, t, :], writes=[(c.X, t)])
        L = [make_layer(c, l) for l in layers]
        L[0].begin()
        skew([lambda t: xt_transpose(c, t), lambda t: xt_evac(c, t), L[0].hook])
        for li, ly in enumerate(L):
            last = (li == len(L) - 1)
            ly.main()
            nxt = None if last else L[li + 1]
            layer_norm(c, ly.gl, nxt, seq)
    st = P.finalize()
    return nc, st


def make_layer(c, l):
    j = int(l[1])
    return AttnLayer(c, j) if l[0] == "A" else PoolLayer(c, j)


def dbg_tap(c, name, tile, ap, shape, dtype):
    d = c.nc.dram_tensor(name, list(shape), dtype, kind="ExternalOutput").ap()
    c.P.dma("sp", d, ap, reads=[tile], writes=["dbg_" + name])


def next_ps(c):
    t = c.PS[c.ps_i % 6]
    c.ps_i += 1
    return t


def next_ws(c):
    t = c.WS[c.ws_i % NWS]
    c.ws_i += 1
    return t


def evac(c, out_ap, in_ap, reads, writes, scale=None, eng=None):
    c.ev_i += 1
    if eng is None:
        eng = "act" if c.ev_i % 2 == 0 else "dve"
    if eng == "act":
        if scale is None:
            c.P.op("act", "activation", out=out_ap, in_=in_ap, func=AF.Copy, reads=reads, writes=writes)
        else:
            c.P.op("act", "activation", out=out_ap, in_=in_ap, func=AF.Copy, scale=scale,
                   reads=reads, writes=writes)
    else:
        if scale is None:
            c.P.op("dve", "tensor_copy", out_ap, in_ap, reads=reads, writes=writes)
        else:
            c.P.op("dve", "tensor_scalar_mul", out_ap, in_ap, scale, reads=reads, writes=writes)


def load_w(c, view, src, tile):
    c.P.dma("pool", view, src, writes=[tile])


def xt_transpose(c, t):
    P = c.P
    banks = [c.PS[6], c.PS[7]]
    for k in range(8):
        ps = banks[k // 4]
        P.op("pe", "transpose", ps[:, (k % 4) * 128:(k % 4 + 1) * 128],
             c.X[:, t, k * 128:(k + 1) * 128], c.identF[:],
             reads=[(c.X, t), c.identF], writes=[ps])


def xt_evac(c, t):
    banks = [c.PS[6], c.PS[7]]
    for hf in range(2):
        evac(c, c.XT[:, 4 * hf:4 * hf + 4, t * 128:(t + 1) * 128],
             banks[hf][:].rearrange("p (k q) -> p k q", q=128), [banks[hf]], [(c.XT, t)],
             eng="act")


def skew(stages, n=NT):
    ns = len(stages)
    for i in range(n + ns - 1):
        for sidx in range(ns - 1, -1, -1):
            t = i - sidx
            if 0 <= t < n:
                stages[sidx](t)


def ln_stats(c, t, bank):
    P = c.P
    P.op("act", "activation", out=c.X[:, t, :], in_=c.X[:, t, :], func=AF.Identity,
         accum_out=c.lst[:, t, 0:1], reads=[(c.X, t)], writes=[(c.X, t), (c.lst, t)])
    ps = bank(c)
    for hf in range(2):
        P.op("act", "activation", out=ps[:], in_=c.X[:, t, hf * 512:(hf + 1) * 512], func=AF.Square,
             accum_out=c.lst[:, t, 1 + hf:2 + hf], reads=[(c.X, t)], writes=[ps, (c.lst, t)])


def layer_norm(c, gl, nxt, seq=0):
    P = c.P
    P.dma("sp", c.lng_v, c.lng_d[:, gl, :], writes=[c.LNG])
    P.dma("sp", c.lnb_v, c.lnb_d[:, gl, :], writes=[c.LNB])
    if nxt is not None:
        nxt.begin()
    P.op("dve", "tensor_scalar_mul", c.lmv[:, :, 0:1], c.lst[:, :, 0:1], 1.0 / D,
         reads=[c.lst], writes=[c.lmv])
    P.op("dve", "tensor_tensor", c.lst[:, :, 3:4], c.lst[:, :, 1:2], c.lst[:, :, 2:3], ALU.add,
         reads=[c.lst], writes=[c.lst])
    P.op("dve", "tensor_tensor", c.lst[:, :, 4:5], c.lmv[:, :, 0:1], c.lmv[:, :, 0:1], ALU.mult,
         reads=[c.lmv], writes=[c.lst])
    P.op("dve", "scalar_tensor_tensor", c.lmv[:, :, 1:2], c.lst[:, :, 3:4], 1.0 / D,
         c.lst[:, :, 4:5], ALU.mult, ALU.subtract, reads=[c.lst], writes=[c.lmv])
    P.op("dve", "tensor_scalar_add", c.lmv[:, :, 2:3], c.lmv[:, :, 1:2], LN_EPS,
         reads=[c.lmv], writes=[c.lmv])
    P.op("act", "activation", out=c.lmv[:, :, 3:4], in_=c.lmv[:, :, 2:3], func=AF.Sqrt,
         reads=[c.lmv], writes=[c.lmv])
    P.op("dve", "reciprocal", c.lmv[:, :, 2:3], c.lmv[:, :, 3:4], reads=[c.lmv], writes=[c.lmv])
    P.op("dve", "scalar_tensor_tensor", c.lmv[:, :, 3:4], c.lmv[:, :, 0:1], -1.0,
         c.lmv[:, :, 2:3], ALU.mult, ALU.mult, reads=[c.lmv], writes=[c.lmv])
    def sA(t):
        P.op("act", "activation", out=c.X[:, t, :], in_=c.X[:, t, :], func=AF.Identity,
             scale=c.lmv[:, t, 2:3], bias=c.lmv[:, t, 3:4],
             reads=[(c.X, t), (c.lmv, t)], writes=[(c.X, t)])

    def sB(t):
        P.op("dve", "tensor_tensor", c.X[:, t, :], c.X[:, t, :], c.lng_v, ALU.mult,
             reads=[(c.X, t), c.LNG], writes=[(c.X, t)])

    def sC(t):
        P.op("dve", "tensor_tensor", c.X[:, t, :], c.X[:, t, :], c.lnb_v, ALU.add,
             reads=[(c.X, t), c.LNB], writes=[(c.X, t)])

    def sOut(t):
        ov = c.out[seq].rearrange("(t p) d -> p t d", p=128)
        P.dma("sp", ov[:, t, :], c.X[:, t, :], reads=[(c.X, t)], writes=["out%d_%d" % (seq, t)])

    stages = [sA, sB, sC]
    if nxt is not None:
        stages += [lambda t: xt_transpose(c, t), lambda t: xt_evac(c, t), nxt.hook]
    else:
        stages += [sOut]
    skew(stages)


class PoolLayer:
    def __init__(self, c, j):
        self.c, self.j = c, j
        self.gl = 2 * j + 1
        self.st = {}

    def load(self, g, which):
        c, j = self.c, self.j
        st = self.st.setdefault(g, dict(g=g))
        if which == "wu":
            t = c.WS[0]
            st["wu_t"], st["wu"] = t, t[:].rearrange("p (k n) -> p k n", n=512)
            load_w(c, st["wu"], c.pw_u[j, g], t)
        elif which == "wg":
            t = c.WS[1]
            st["wg_t"], st["wg"] = t, t[:].rearrange("p (k n) -> p k n", n=512)
            load_w(c, st["wg"], c.pw_g[j, g], t)
        elif which == "wgrp":
            t = c.WSH
            st["wgrp_t"], st["wgrp"] = t, t[:].rearrange("p (k n) -> p k n", n=512)
            load_w(c, st["wgrp"], c.pw_grp[j, g], t)
        else:
            t = c.WS[2]
            st["wout_t"], st["wout"] = t, t[:].rearrange("p (k n) -> p k n", n=1024)
            load_w(c, st["wout"], c.pw_out[j, g], t)

    def begin(self):
        for w in ("wu", "wg", "wgrp", "wout"):
            self.load(0, w)

    def hook(self, t):
        self.u_tile(0, t)
        if t % 4 == 3:
            self.gate_chunk(0, t // 4)

    def u_tile(self, g, t):
        c, P, st = self.c, self.c.P, self.st[g]
        UT = c.AR[0:4]
        ps = next_ps(c)
        for k in range(8):
            P.op("pe", "matmul", ps[:], c.XT[:, k, t * 128:(t + 1) * 128], st["wu"][:, k, :],
                 start=(k == 0), stop=(k == 7), reads=[(c.XT, t), st["wu_t"]], writes=[ps])
        ut = UT[t // 4]
        evac(c, ut[:, (t % 4) * 512:(t % 4 + 1) * 512], ps[:], [ps], [ut])

    def gate_chunk(self, g, q):
        c, P, st = self.c, self.c.P, self.st[g]
        ZT = c.AR[8:12]
        for et in range(4):
            psg = next_ps(c)
            for k in range(8):
                P.op("pe", "matmul", psg[:], st["wg"][:, k, et * 128:(et + 1) * 128],
                     c.XT[:, k, q * 512:(q + 1) * 512], start=(k == 0), stop=(k == 7),
                     reads=[st["wg_t"], (c.XT, range(4 * q, 4 * q + 4))], writes=[psg])
            P.op("act", "activation", out=ZT[et][:, q * 512:(q + 1) * 512], in_=psg[:], func=AF.Silu,
                 reads=[psg], writes=[ZT[et]])

    def rest(self, g):
        c, P, st, j = self.c, self.c.P, self.st[g], self.j
        UT, PT, ZT = c.AR[0:4], c.AR[4:8], c.AR[8:12]
        wgrp, wout = st["wgrp"], st["wout"]
        for ct in range(4):
            for q in range(4):
                ps = next_ps(c)
                for i in range(4):
                    tt = 4 * q + i
                    ut = UT[tt // 4]
                    lhs = ut[:, (tt % 4) * 512 + ct * 128:(tt % 4) * 512 + (ct + 1) * 128]
                    o = ps[:, i * 128:(i + 1) * 128]
                    if tt == 0:
                        P.op("pe", "matmul", o, lhs, c.first[:, g, 0, :], start=True, stop=False,
                             reads=[ut, c.first], writes=[ps])
                        P.op("pe", "matmul", o, lhs, c.first[:, g, 1, :], start=False, stop=True,
                             reads=[ut, c.first], writes=[ps])
                    else:
                        P.op("pe", "matmul", o, lhs, c.band[:, g, 0:128], start=True, stop=False,
                             reads=[ut, c.band], writes=[ps])
                        up = UT[(tt - 1) // 4]
                        lhp = up[:, ((tt - 1) % 4) * 512 + ct * 128:((tt - 1) % 4) * 512 + (ct + 1) * 128]
                        P.op("pe", "matmul", o[:, 0:16], lhp, c.band[:, g, 128:144],
                             start=False, stop=True, reads=[up, c.band], writes=[ps])
                evac(c, PT[ct][:, q * 512:(q + 1) * 512], ps[:], [ps], [PT[ct]])
        if g + 1 < 4:
            self.load(g + 1, "wu")
            self.load(g + 1, "wg")
        for et in range(4):
            ge = 4 * g + et
            for q in range(4):
                psy = next_ps(c)
                for ct in range(4):
                    P.op("pe", "matmul", psy[:], wgrp[:, ct, et * 128:(et + 1) * 128],
                         PT[ct][:, q * 512:(q + 1) * 512], start=(ct == 0), stop=(ct == 3),
                         reads=[st["wgrp_t"], PT[ct]], writes=[psy])
                zs = ZT[et][:, q * 512:(q + 1) * 512]
                P.op("dve", "scalar_tensor_tensor", zs, psy[:], c.psc[:, j, ge:ge + 1], zs,
                     ALU.mult, ALU.mult, reads=[psy, ZT[et], c.psc], writes=[ZT[et]])
        if g + 1 < 4:
            self.load(g + 1, "wgrp")
        for t in range(NT):
            for nh in range(2):
                ps = next_ps(c)
                for et in range(4):
                    P.op("pe", "matmul", ps[:], ZT[et][:, t * 128:(t + 1) * 128],
                         wout[:, et, nh * 512:(nh + 1) * 512], start=(et == 0), stop=(et == 3),
                         reads=[ZT[et], st["wout_t"]], writes=[ps])
                xo = c.X[:, t, nh * 512:(nh + 1) * 512]
                if g == 0:
                    P.op("dve", "scalar_tensor_tensor", xo, xo, ALPHA, ps[:], ALU.mult, ALU.add,
                         reads=[(c.X, t), ps], writes=[(c.X, t)])
                else:
                    P.op("dve", "tensor_tensor", xo, xo, ps[:], ALU.add,
                         reads=[(c.X, t), ps], writes=[(c.X, t)])
            if g == 3:
                ln_stats(c, t, next_ps)
        if g + 1 < 4:
            self.load(g + 1, "wout")

    def main(self):
        for g in range(4):
            if g > 0:
                for t in range(NT):
                    self.u_tile(g, t)
                    if t % 4 == 3:
                        self.gate_chunk(g, t // 4)
            self.rest(g)


def attn_setup(c):
    P = c.P
    c.past = P.tile("past", [128, 128], F32)
    c.notown = P.tile("notown", [128, 128], F32)
    c.abf = P.tile("abf", [8, 2], F32)
    c.wff = P.tile("wff", [128, 8, 8], BF16)
    c.gm = [P.tile("gm", [128, 128], F32) for _ in range(2)]
    c.mx = [P.tile("mx", [128, 128], F32) for _ in range(2)]
    c.sel = [P.tile("sel", [128, 128], F32) for _ in range(2)]
    c.mb = [P.tile("mb", [128, 128], BF16) for _ in range(2)]
    c.mm = [P.tile("mm", [128, 3, 16], F32) for _ in range(2)]
    c.km = P.tile("km", [128, 8], F32)
    c.kmb = [P.tile("kmb", [64, 8], BF16) for _ in range(2)]
    c.rden = [P.tile("rden", [128, 4], F32) for _ in range(2)]
    P.dma("sp", c.past[:], c.cpast_d.rearrange("p a b -> p (a b)"), writes=[c.past])
    P.dma("sp", c.notown[:], c.cnotown_d.rearrange("p a b -> p (a b)"), writes=[c.notown])
    P.dma("sp", c.abf[:], c.ab_f[:, :], writes=[c.abf])
    c.fox_al = [c.AR[i].alias() for i in (4, 5, 6, 7)]
    c.pa_i = 0
    c.psS_i = 0
    c.po_i = 0
    c.pt_i = 0
    c.rd_i = 0


def ps_a(c):
    t = c.PS[c.pa_i % 4]
    c.pa_i += 1
    return t


def ps_s(c):
    t = c.PS[4 + c.psS_i % 2]
    c.psS_i += 1
    return t


def ps_o(c):
    t = c.PS[6 + c.po_i % 2]
    c.po_i += 1
    return t


def vp_sgt(c, u):
    VP_t, SGT_t = (c.AR[8], c.AR[9]) if u % 2 == 0 else (c.AR[15], c.AR[16])
    VP = VP_t[:, 0:2080].rearrange("p (t h d) -> p t h d", h=2, d=65)
    SGT = SGT_t[:, 0:2048].rearrange("p (t d) -> p t d", d=128)
    return VP_t, SGT_t, VP, SGT


class AttnLayer:
    def __init__(self, c, j):
        self.c, self.j = c, j
        self.gl = 2 * j
        self.pairs = {}
        self.fq, self.fgen = 0, None

    def fox_gen(self, q):
        c, P, j = self.c, self.c.P, self.j
        CC = c.AR[14]
        tA, tM, tC, tR = c.fox_al
        R0 = slice(96, 104)
        A = tA[R0, 0:1024].bitcast(F32)
        M = tM[R0, 0:1024].bitcast(F32)
        Cc = tC[R0, 0:1024].bitcast(F32)
        R = tR[R0, 0:1024].bitcast(F32)
        fcar = tM[R0, 2048:2050].bitcast(F32)
        c1t = tA[R0, 1024:1536]
        T2 = tA[R0, 1536:2048]
        bf = c.abf[:, j:j + 1]
        if q == 0:
            P.dma("pool", c.wff[:], c.aw_ff[j], writes=[c.wff])
        ps = ps_a(c)
        for k in range(8):
            P.op("pe", "matmul", ps[0:8, :], c.wff[:, k, :], c.XT[:, k, q * 512:(q + 1) * 512],
                 start=(k == 0), stop=(k == 7),
                 reads=[c.wff, (c.XT, range(4 * q, 4 * q + 4))], writes=[ps])
        P.op("act", "activation", out=A, in_=ps[0:8, :], func=AF.Abs, bias=bf,
             reads=[ps, c.abf], writes=[tA])
        P.op("dve", "tensor_scalar", M, ps[0:8, :], bf, 0.0, ALU.add, ALU.min,
             reads=[ps, c.abf], writes=[tM])
        yield
        P.op("act", "activation", out=A, in_=A, func=AF.Exp, scale=-1.0, reads=[tA], writes=[tA])
        yield
        P.op("act", "activation", out=A, in_=A, func=AF.Ln, bias=1.0, reads=[tA], writes=[tA])
        yield
        P.op("dve", "tensor_tensor", M, M, A, ALU.subtract, reads=[tM, tA], writes=[tM])
        yield
        P.op("dve", "tensor_scalar_mul", M, M, 0.5, reads=[tM], writes=[tM])
        yield
        init = 0.0 if q == 0 else fcar
        P.op("dve", "tensor_tensor_scan", Cc, M, M, init, ALU.add, ALU.add,
             reads=[tM], writes=[tC])
        yield
        if q < 3:
            P.op("dve", "tensor_copy", fcar, Cc[:, 511:512], reads=[tC], writes=[tM])
        hs = slice(q * 512, (q + 1) * 512)
        P.op("dve", "tensor_copy", c1t, Cc, reads=[tC], writes=[tA])
        yield
        P.op("dve", "tensor_copy", CC[0:8, hs], c1t, reads=[tA], writes=[CC])
        P.op("dve", "tensor_scalar_mul", CC[32:40, hs], c1t, -1.0, reads=[tA], writes=[CC])
        P.op("dve", "tensor_tensor", R, Cc, c1t, ALU.subtract, reads=[tC, tA], writes=[tR])
        yield
        P.op("dve", "tensor_copy", T2, R, reads=[tR], writes=[tA])
        yield
        P.op("dve", "tensor_scalar_mul", CC[64:72, hs], T2, -1.0, reads=[tA], writes=[CC])
        P.op("dve", "tensor_tensor", R, R, T2, ALU.subtract, reads=[tR, tA], writes=[tR])
        yield
        P.op("dve", "tensor_scalar_mul", CC[96:104, hs], R, -1.0, reads=[tR], writes=[CC])
        yield

    def fox_pull(self, tmax, n):
        while n > 0 and self.fq < 4 and 4 * self.fq + 3 <= tmax:
            if self.fgen is None:
                self.fgen = self.fox_gen(self.fq)
            try:
                next(self.fgen)
                n -= 1
            except StopIteration:
                self.fgen = None
                self.fq += 1

    def pair_begin(self, u):
        c, j = self.c, self.j
        w_t = c.WS[u % 2]
        w = w_t[:].rearrange("p (k n) -> p k n", n=512)
        load_w(c, w, c.aw_in[j, u], w_t)
        base = 4 * (u % 2)
        AR = c.AR
        st = dict(u=u, fox=(u < 4), KA=(68 if u < 4 else 72), w=w, w_t=w_t,
                  QT=[AR[base + 0], AR[base + 1]], KT=[AR[base + 2], AR[base + 3]])
        self.pairs[u] = st
        return st

    def qk_gen(self, u, q, ev="dve"):
        c, P, st = self.c, self.c.P, self.pairs[u]
        w, w_t, QT, KT, fox = st["w"], st["w_t"], st["QT"], st["KT"], st["fox"]
        ps = ps_a(c)
        for k in range(8):
            P.op("pe", "matmul", ps[:], w[:, k, 0:128], c.XT[:, k, q * 512:(q + 1) * 512],
                 start=(k == 0), stop=(k == 7),
                 reads=[w_t, (c.XT, range(4 * q, 4 * q + 4))], writes=[ps])
            if k < 7:
                yield
        for hh in range(2):
            evac(c, QT[hh][0:64, q * 512:(q + 1) * 512], ps[hh * 64:(hh + 1) * 64, :],
                 [ps], [QT[hh]], scale=0.125, eng=ev)
        yield
        ps = ps_a(c)
        for k in range(8):
            P.op("pe", "matmul", ps[:], w[:, k, 128:256], c.XT[:, k, q * 512:(q + 1) * 512],
                 start=(k == 0), stop=(k == 7),
                 reads=[w_t, (c.XT, range(4 * q, 4 * q + 4))], writes=[ps])
            if k < 7:
                yield
        for hh in range(2):
            evac(c, KT[hh][0:64, q * 512:(q + 1) * 512], ps[hh * 64:(hh + 1) * 64, :],
                 [ps], [KT[hh]], eng=(ev if fox else "dve"))
        if not fox:
            P.op("dve", "tensor_reduce", c.km[:, 2 * q:2 * q + 2],
                 ps[:].rearrange("p (a b) -> p a b", b=256), AX.X, ALU.add,
                 reads=[ps], writes=[c.km])
        yield

    def qk_chunk(self, u, q):
        for _ in self.qk_gen(u, q, ev="dve"):
            pass

    def vg_gen(self, u, t4, ev="dve"):
        c, P, st = self.c, self.c.P, self.pairs[u]
        w, w_t = st["w"], st["w_t"]
        VP_t, SGT_t, VP, SGT = vp_sgt(c, u)
        for part in range(2):
            ps = ps_a(c)
            col = 256 if part == 0 else 384
            for ti in range(4):
                t = 4 * t4 + ti
                for k in range(8):
                    P.op("pe", "matmul", ps[:, ti * 128:(ti + 1) * 128],
                         c.XT[:, k, t * 128:(t + 1) * 128], w[:, k, col:col + 128],
                         start=(k == 0), stop=(k == 7), reads=[w_t, (c.XT, t)], writes=[ps])
                    if k % 4 == 3 and not (ti == 3 and k == 7):
                        yield
            if part == 0:
                evac(c, VP[:, 4 * t4:4 * t4 + 4, :, 0:64],
                     ps[:].rearrange("p (t h d) -> p t h d", h=2, d=64), [ps], [VP_t],
                     scale=0.5, eng=ev)
            else:
                sg = SGT[:, 4 * t4:4 * t4 + 4, :]
                g3 = ps[:].rearrange("p (t d) -> p t d", d=128)
                P.op("act", "activation", out=sg, in_=g3, func=AF.Tanh, scale=0.5,
                     reads=[ps], writes=[SGT_t])
                P.op("dve", "scalar_tensor_tensor", sg, sg, 1.0, g3, ALU.add, ALU.mult,
                     reads=[SGT_t, ps], writes=[SGT_t])
            yield

    def vg_group(self, u, t4):
        for _ in self.vg_gen(u, t4, ev="act"):
            pass

    def aug_start(self, u):
        c, P, st = self.c, self.c.P, self.pairs[u]
        QT, KT, fox = st["QT"], st["KT"], st["fox"]
        CC = c.AR[14]
        if fox:
            for hh in range(2):
                hf = 2 * (u % 4) + hh
                P.op("pool", "memset", QT[hh][64:68, 0:2048], 1.0, writes=[QT[hh]])
                P.op("pool", "memset", KT[hh][64:68, 0:2048], 1.0, writes=[KT[hh]])
                P.dma("sp", QT[hh][64:65, 0:2048], CC[hf:hf + 1, 0:2048], reads=[CC], writes=[QT[hh]])
                for r in range(3):
                    P.dma("sp", KT[hh][65 + r:66 + r, 0:2048],
                          CC[32 * (r + 1) + hf:32 * (r + 1) + hf + 1, 0:2048],
                          reads=[CC], writes=[KT[hh]])
        else:
            for hh in range(2):
                P.op("dve", "tensor_scalar_mul", c.kmb[hh][:], c.km[hh * 64:(hh + 1) * 64, :],
                     1.0 / 256.0, reads=[c.km], writes=[c.kmb[hh]])
                P.dma("pool", KT[hh][64:72, 0:2048], c.cblk_d[:, :], writes=[KT[hh]])

    def moba_gate_a_gen(self, u, hh):
        c, P, st = self.c, self.c.P, self.pairs[u]
        QT_t, kmb = st["QT"][hh], c.kmb[hh]
        gm, g2, e, mb, mm = c.gm[hh], c.mx[hh], c.sel[hh], c.mb[hh], c.mm[hh]
        ps = ps_a(c)
        for i in range(16):
            P.op("pe", "matmul", ps[:, i * 8:(i + 1) * 8], QT_t[0:64, i * 128:(i + 1) * 128], kmb[:],
                 start=True, stop=True, reads=[QT_t, kmb], writes=[ps])
            if i % 4 == 3:
                yield
        P.op("dve", "tensor_tensor", gm[:], ps[:, 0:128], c.past[:], ALU.add,
             reads=[ps, c.past], writes=[gm])
        yield

        def v3(t):
            return t[:].rearrange("p (a b) -> p a b", b=8)

        def bc(k):
            return mm[:, k, :].unsqueeze(2).to_broadcast([128, 16, 8])

        KO = -1e32
        P.op("dve", "tensor_reduce", mm[:, 0, :], v3(gm), AX.X, ALU.max, reads=[gm], writes=[mm])
        yield
        P.op("dve", "tensor_tensor", v3(e), v3(gm), bc(0), ALU.is_ge, reads=[gm, mm], writes=[e])
        yield
        P.op("dve", "scalar_tensor_tensor", g2[:], e[:], KO, gm[:], ALU.mult, ALU.add,
             reads=[e, gm], writes=[g2])
        yield
        P.op("dve", "tensor_reduce", mm[:, 1, :], v3(g2), AX.X, ALU.max, reads=[g2], writes=[mm])
        yield
        P.op("dve", "tensor_tensor", v3(e), v3(g2), bc(1), ALU.is_ge, reads=[g2, mm], writes=[e])
        yield
        P.op("dve", "scalar_tensor_tensor", g2[:], e[:], KO, g2[:], ALU.mult, ALU.add,
             reads=[e, g2], writes=[g2])
        yield
        P.op("dve", "tensor_reduce", mm[:, 2, :], v3(g2), AX.X, ALU.max, reads=[g2], writes=[mm])
        yield
        P.op("dve", "tensor_tensor", v3(e), v3(gm), bc(2), ALU.is_ge, reads=[gm, mm], writes=[e])
        yield
        P.op("dve", "scalar_tensor_tensor", mb[:], e[:], -1.0, c.notown[:], ALU.add, ALU.mult,
             reads=[e, c.notown], writes=[mb])
        yield

    def moba_gate_a(self, u, hh):
        for _ in self.moba_gate_a_gen(u, hh):
            pass

    def moba_gate_b(self, u, hh):
        c, P, st = self.c, self.c.P, self.pairs[u]
        QT_t, mb = st["QT"][hh], c.mb[hh]
        MBT_t = c.AR[14]
        MBT = MBT_t[32 * hh:32 * hh + 8, 0:2048]
        for half in range(2):
            pt = ps_a(c)
            ptb = pt[:].bitcast(BF16)
            for ii in range(8):
                i = 8 * half + ii
                P.op("pe", "transpose", ptb[0:8, ii * 128:(ii + 1) * 128], mb[:, i * 8:(i + 1) * 8],
                     c.ident[:], reads=[mb, c.ident], writes=[pt])
            evac(c, MBT[:, half * 1024:(half + 1) * 1024], ptb[0:8, :], [pt], [MBT_t], eng="dve")
        P.dma("sp", QT_t[64:72, 0:2048], MBT, reads=[MBT_t], writes=[QT_t])

    def core_chunk(self, u, hh, qc, filler=None):
        c, P, st = self.c, self.c.P, self.pairs[u]
        QT_t, KT_t, KA = st["QT"][hh], st["KT"][hh], st["KA"]
        YP_t, PTS_t = c.AR[10], c.AR[13]
        VP_t, SGT_t, VP, SGT = vp_sgt(c, u)
        YP = YP_t[:, 0:2048].rearrange("p (t d) -> p t d", d=128)
        po_t = ps_o(c)
        po = po_t[:, 0:260].rearrange("p (i d) -> p i d", d=65)
        nj = 4 * qc + 4
        pts = {}

        def s_tile(jk):
            n0 = max(128 * jk, 512 * qc)
            N = 512 * qc + 512 - n0
            ps = ps_s(c)
            diag = jk >= 4 * qc
            P.op("pe", "matmul", ps[:, 0:N], KT_t[0:KA, jk * 128:(jk + 1) * 128],
                 QT_t[0:KA, n0:n0 + N], start=True, stop=(not diag),
                 reads=[KT_t, QT_t], writes=[ps])
            if diag:
                P.op("pe", "matmul", ps[:, 0:128], c.ident[:], c.causal[:], start=False, stop=True,
                     reads=[c.ident, c.causal], writes=[ps])
            pi = c.pt_i % 4
            c.pt_i += 1
            pt = PTS_t[:, pi * 512:pi * 512 + N]
            P.op("act", "activation", out=pt, in_=ps[:, 0:N], func=AF.Exp,
                 reads=[ps], writes=[(PTS_t, pi)])
            pts[jk] = (pt, pi, n0, N)

        def pv(jk):
            pt, pi, n0, N = pts.pop(jk)
            for ii in range(N // 128):
                qi = n0 // 128 + ii
                P.op("pe", "matmul", po[:, qi % 4, :], pt[:, ii * 128:(ii + 1) * 128],
                     VP[:, jk, hh, :], start=(jk == 0 and ii == 0), stop=(jk == qi),
                     skip_group_check=True,
                     reads=[(PTS_t, pi), VP_t], writes=[po_t])

        LA = 1
        for jk in range(min(LA, nj)):
            s_tile(jk)
        for jk in range(nj):
            if jk + LA < nj:
                s_tile(jk + LA)
            pv(jk)
            if filler is not None:
                next(filler, None)
                next(filler, None)
        rd = c.rden[c.rd_i % 2]
        c.rd_i += 1
        P.op("dve", "reciprocal", rd[:], po[:, :, 64:65], reads=[po_t], writes=[rd])
        yv = YP[:, 4 * qc:4 * qc + 4, hh * 64:(hh + 1) * 64]
        P.op("dve", "tensor_tensor", yv, po[:, :, 0:64],
             rd[:, :].unsqueeze(2).to_broadcast([128, 4, 64]), ALU.mult,
             reads=[po_t, rd], writes=[YP_t])
        P.op("dve", "tensor_tensor", yv, yv, SGT[:, 4 * qc:4 * qc + 4, hh * 64:(hh + 1) * 64], ALU.mult,
             reads=[YP_t, SGT_t], writes=[YP_t])

    def pair_finish(self, u):
        c, P, j = self.c, self.c.P, self.j
        YP_t = c.AR[10]
        YP = YP_t[:, 0:2048].rearrange("p (t d) -> p t d", d=128)
        YGT = c.AR[11:13]
        yg = YGT[u % 2]
        for half in range(2):
            ps = ps_a(c)
            psb = ps[:].bitcast(BF16)
            for ti in range(8):
                t = 8 * half + ti
                P.op("pe", "transpose", psb[:, ti * 128:(ti + 1) * 128], YP[:, t, :], c.ident[:],
                     reads=[YP_t, c.ident], writes=[ps])
            evac(c, yg[:, half * 1024:(half + 1) * 1024], psb, [ps], [yg])
        if u % 2 == 1:
            sgi = u // 2
            wo_t = c.WS[2] if sgi % 2 == 0 else c.WSH
            wo = wo_t[:, 0:2048].rearrange("p (k n) -> p k n", n=1024)
            load_w(c, wo, c.aw_out[j][:, 2 * sgi:2 * sgi + 2, :], wo_t)
            self.wo = (sgi, wo, wo_t)

    def outproj_gen(self):
        c, P = self.c, self.c.P
        sgi, wo, wo_t = self.wo
        YGT = c.AR[11:13]
        for t in range(NT):
            for nh in range(2):
                ps = ps_a(c)
                for pp in range(2):
                    P.op("pe", "matmul", ps[:], YGT[pp][:, t * 128:(t + 1) * 128],
                         wo[:, pp, nh * 512:(nh + 1) * 512], start=(pp == 0), stop=(pp == 1),
                         reads=[YGT[pp], wo_t], writes=[ps])
                xo = c.X[:, t, nh * 512:(nh + 1) * 512]
                if sgi == 0:
                    P.op("dve", "scalar_tensor_tensor", xo, xo, ALPHA, ps[:], ALU.mult, ALU.add,
                         reads=[(c.X, t), ps], writes=[(c.X, t)])
                else:
                    P.op("dve", "tensor_tensor", xo, xo, ps[:], ALU.add,
                         reads=[(c.X, t), ps], writes=[(c.X, t)])
                if not (sgi == 3 and nh == 1):
                    yield
            if sgi == 3:
                ln_stats(c, t, ps_a)
                yield

    def begin(self):
        c, P, j = self.c, self.c.P, self.j
        self.pair_begin(0)
        self.pair_begin(1)
        for par in range(2):
            VP_t, _, VP, _ = vp_sgt(c, par)
            P.op("pool", "memset", VP[:, :, :, 64:65], 1.0, writes=[VP_t])

    def hook(self, t):
        if t % 4 == 3:
            q = t // 4
            self.qk_chunk(0, q)
            self.vg_group(0, q)
        self.fox_pull(t, 3)

    def next_pair_filler(self, u1):
        if u1 < 8:
            for q in range(4):
                yield from self.qk_gen(u1, q)
            self.aug_start(u1)
            self.aug_done = u1
            yield
            if not self.pairs[u1]["fox"]:
                yield from self.moba_gate_a_gen(u1, 0)
                yield from self.moba_gate_a_gen(u1, 1)
                yield
                self.moba_gate_b(u1, 0)
                yield
                self.moba_gate_b(u1, 1)
                self.gate_b_done = u1
                yield
        if u1 % 2 == 1 and u1 >= 3:
            yield from self.outproj_gen()
        if u1 < 8:
            for t4 in range(4):
                yield from self.vg_gen(u1, t4)

    def main(self):
        self.fox_pull(NT - 1, 10 ** 6)
        self.aug_start(0)
        for u in range(8):
            if u + 2 < 8:
                self.pair_begin(u + 2)
            filler = self.next_pair_filler(u + 1)
            self.step_i = 0
            self.heavy = ((u + 1) % 2 == 1 and u + 1 >= 3)
            for hh in range(2):
                for qc in range(4):
                    self.core_chunk(u, hh, qc, filler)
            if filler is not None:
                for _ in filler:
                    pass
            nm = (u + 1 < 8) and not self.pairs[u + 1]["fox"]
            self.pair_finish(u)
            if u == 7:
                for _ in self.outproj_gen():
                    pass


def _consts():
    import ml_dtypes
    bf = ml_dtypes.bfloat16
    cs = {}
    cs["c_ident"] = np.eye(128, dtype=np.float32)
    k = np.arange(128)[:, None]
    q = np.arange(128)[None, :]
    cs["c_causal"] = np.where(k > q, NEG, 0.0).astype(np.float32)
    band = np.zeros((128, 4, 144), np.float32)
    first = np.zeros((128, 4, 2, 128), np.float32)
    s = np.arange(128)[:, None]
    for g, w in enumerate((2, 4, 8, 16)):
        tl = np.arange(144)[None, :]
        m = ((tl - s >= 0) & (tl - s <= w - 1)).astype(np.float32) / w
        m = m - (tl == s).astype(np.float32)
        band[:, g, :] = m
        tl = np.arange(128)[None, :]
        cnt = np.minimum(tl + 1, w).astype(np.float64)
        m = ((tl - s >= 0) & (tl - s <= w - 1)).astype(np.float64) / cnt - (tl == s)
        hi = m.astype(np.float32).astype(bf).astype(np.float32)
        lo = (m - hi).astype(np.float32).astype(bf).astype(np.float32)
        first[:, g, 0, :] = hi
        first[:, g, 1, :] = lo
    cs["c_band"] = band
    cs["c_first"] = first
    kk = np.arange(2048)
    cs["c_blk"] = (kk[None, :] // 256 == np.arange(8)[:, None]).astype(np.float32)
    qt = np.arange(16)[None, :, None]
    n = np.arange(8)[None, None, :]
    past = np.where(n < qt // 2, 0.0, -1e30).astype(np.float32)
    cs["c_past"] = np.broadcast_to(past, (128, 16, 8)).copy()
    cs["c_notown"] = np.broadcast_to((n != qt // 2).astype(np.float32) * (-NEG), (128, 16, 8)).copy()
    return cs


def _kd(w):
    K = w.shape[0] // 128
    return np.ascontiguousarray(w.reshape(K, 128, w.shape[1]).transpose(1, 0, 2))


def prep_weights(attn_w_in, attn_b_f, attn_w_out, pool_w_in, pool_w_grp, pool_scale,
                 pool_w_out, ln_g, ln_b):
    f = np.float32
    W = {}
    aw_in = np.zeros((2, 8, 128, 8, 512), f)
    aw_ff = np.zeros((2, 128, 8, 8), f)
    aw_out = np.zeros((2, 128, 8, 1024), f)
    for j in range(2):
        w = np.asarray(attn_w_in[j], f)
        fq, fk, fv, fg = w[:, 0:512], w[:, 512:1024], w[:, 1024:1536], w[:, 1536:2048]
        ff = w[:, 2048:2056]
        mq, mk, mv, mg = (w[:, 2056:2568], w[:, 2568:3080], w[:, 3080:3592], w[:, 3592:4104])
        for u in range(8):
            if u < 4:
                q_, k_, v_, g_ = fq, fk, fv, fg
                p = u
            else:
                q_, k_, v_, g_ = mq, mk, mv, mg
                p = u - 4
            sl = slice(p * 128, (p + 1) * 128)
            cat = np.concatenate([q_[:, sl], k_[:, sl], v_[:, sl], g_[:, sl]], axis=1)
            aw_in[j, u] = _kd(cat)
        aw_ff[j] = _kd(ff)
        aw_out[j] = _kd(np.asarray(attn_w_out[j], f))
    W["aw_in"], W["aw_ff"], W["aw_out"] = aw_in, aw_ff, aw_out
    W["ab_f"] = np.ascontiguousarray(np.asarray(attn_b_f, f).T)
    pw_u = np.zeros((2, 4, 128, 8, 512), f)
    pw_g = np.zeros((2, 4, 128, 8, 512), f)
    pw_grp = np.zeros((2, 4, 128, 4, 512), f)
    pw_out = np.zeros((2, 4, 128, 4, 1024), f)
    for j in range(2):
        w = np.asarray(pool_w_in[j], f)
        for g in range(4):
            pw_u[j, g] = _kd(w[:, g * 512:(g + 1) * 512])
            pw_g[j, g] = _kd(w[:, 2048 + g * 512:2048 + (g + 1) * 512])
            pw_grp[j, g] = _kd(np.asarray(pool_w_grp[j, g], f))
            pw_out[j, g] = _kd(np.asarray(pool_w_out[j], f)[g * 512:(g + 1) * 512])
    W["pw_u"], W["pw_g"], W["pw_grp"], W["pw_out"] = pw_u, pw_g, pw_grp, pw_out
    W["psc"] = np.ascontiguousarray(np.asarray(pool_scale, f).reshape(2, 16, 128).transpose(2, 0, 1))
    W["lng"] = np.ascontiguousarray(np.broadcast_to(np.asarray(ln_g, f)[None], (128, 4, 1024)))
    W["lnb"] = np.ascontiguousarray(np.broadcast_to(np.asarray(ln_b, f)[None], (128, 4, 1024)))
    W.update(_consts())
    return W


_CACHE = {}


def run(x, W, layers=("A0", "P0", "A1", "P1"), nseq=2, ncores=8, trace=False):
    key = (tuple(layers), nseq)
    if key not in _CACHE:
        _CACHE[key] = build_program(layers, nseq)
    nc, st = _CACHE[key]
    in_maps = []
    for i in range(ncores):
        m = dict(W)
        m["x"] = np.ascontiguousarray(x[2 * i:2 * i + 2])
        in_maps.append(m)
    res = run_bass_kernel_spmd(nc, in_maps, core_ids=list(range(ncores)), trace=trace)
    out = np.concatenate([r["out"] for r in res.results], axis=0)
    return out, res


def kernel(x, attn_w_in, attn_b_f, attn_w_out, pool_w_in, pool_w_grp, pool_scale,
           pool_w_out, ln_g, ln_b):
    W = prep_weights(attn_w_in, attn_b_f, attn_w_out, pool_w_in, pool_w_grp, pool_scale,
                     pool_w_out, ln_g, ln_b)
    out, _ = run(np.asarray(x, np.float32), W)
    return out.astype(np.float32)
```

```python
from concourse.bass_utils import run_bass_kernel_spmd
import numpy as np
import concourse.bass as bass
import concourse.mybir as mybir

F32 = mybir.dt.float32
BF16 = mybir.dt.bfloat16
AF = mybir.ActivationFunctionType
ALU = mybir.AluOpType
AX = mybir.AxisListType

COMPUTE = ("pe", "act", "dve", "pool")
EPOCH = 12000


class T:
    _n = 0

    def __init__(self, P, name, shape, dtype, space="sbuf", parts=1):
        T._n += 1
        self.id = T._n
        self.name = name
        self.parts = parts
        self.space = space
        nc = P.nc
        nm = "%s_%d" % (name, self.id)
        if space == "sbuf":
            self.h = nc.alloc_sbuf_tensor(nm, list(shape), dtype)
        else:
            self.h = nc.alloc_psum_tensor(nm, list(shape), dtype)

    def __getitem__(self, idx):
        return self.h[idx]

    def alias(self):
        a = T.__new__(T)
        T._n += 1
        a.id, a.name, a.parts, a.space, a.h = T._n, self.name + "_al", 1, self.space, self.h
        return a

    def keys(self, parts=None):
        if parts is None:
            return [(self.id, p) for p in range(self.parts)]
        if isinstance(parts, int):
            return [(self.id, parts)]
        return [(self.id, p) for p in parts]


def _psum_keys(lst):
    out = []
    for x in lst:
        t = x[0] if isinstance(x, tuple) else x
        if isinstance(t, T) and t.space == "psum":
            out += t.keys()
    return out


def _keys(lst):
    out = []
    for x in lst:
        if x is None:
            continue
        if isinstance(x, T):
            out += x.keys()
        elif isinstance(x, tuple) and isinstance(x[0], T):
            out += x[0].keys(x[1])
        else:
            out.append(("d", x))
    return out


class Op:
    __slots__ = ("eng", "fn", "reads", "writes", "dma", "deps", "signal", "seq",
                 "sigval", "sem", "waits", "slot", "prevwait")


class Prog:
    def __init__(self, nc, n_dma_sems=24):
        self.nc = nc
        self.ops = []
        self.eng = {"pe": nc.tensor, "act": nc.scalar, "dve": nc.vector,
                    "pool": nc.gpsimd, "sp": nc.sync}
        self.n_dma_sems = n_dma_sems

    def tile(self, name, shape, dtype, space="sbuf", parts=1):
        return T(self, name, shape, dtype, space, parts)

    def op(self, eng, name, *args, reads=(), writes=(), **kw):
        o = Op()
        o.eng = eng
        o.fn = (lambda e, name=name, args=args, kw=kw: getattr(e, name)(*args, **kw))
        o.reads = _keys(reads)
        o.writes = _keys(writes) + _psum_keys(reads)
        o.dma = False
        self.ops.append(o)
        return o

    def dma(self, q, out, in_, reads=(), writes=()):
        o = Op()
        o.eng = q
        o.fn = (lambda e, out=out, in_=in_: e.dma_start(out=out, in_=in_))
        o.reads = _keys(reads)
        o.writes = _keys(writes)
        o.dma = True
        self.ops.append(o)
        return o

    def finalize(self):
        nc = self.nc
        ops = self.ops
        last_w = {}
        readers = {}
        for i, o in enumerate(ops):
            deps = set()
            for k in o.reads:
                j = last_w.get(k)
                if j is not None:
                    deps.add(j)
            for k in o.writes:
                j = last_w.get(k)
                if j is not None:
                    deps.add(j)
                for r in readers.get(k, ()):
                    deps.add(r)
            deps.discard(i)
            o.deps = sorted(deps)
            for k in o.reads:
                readers.setdefault(k, []).append(i)
            for k in o.writes:
                last_w[k] = i
                readers[k] = []
        seqc = {e: 0 for e in self.eng}
        known = {e: {} for e in self.eng}
        snap = [None] * len(ops)
        dma_slot_ctr = {e: 0 for e in self.eng}
        slot_last = {}
        for i, o in enumerate(ops):
            E = o.eng
            o.signal = False
            o.waits = []
            kn = known[E]
            o.seq = seqc[E]
            seqc[E] += 1
            for j in o.deps:
                p = ops[j]
                if p.dma:
                    key = ("d", j)
                    if kn.get(key):
                        continue
                    o.waits.append(j)
                    kn = dict(kn)
                    kn[key] = 1
                    for kk, vv in snap[j].items():
                        if kn.get(kk, -1) < vv:
                            kn[kk] = vv
                else:
                    B = p.eng
                    if B == E and E == "pe":
                        continue
                    key = ("c", B)
                    if kn.get(key, -1) >= p.seq:
                        continue
                    o.waits.append(j)
                    p.signal = True
                    kn = dict(kn)
                    kn[key] = p.seq
                    for kk, vv in snap[j].items():
                        if kn.get(kk, -1) < vv:
                            kn[kk] = vv
            o.prevwait = None
            if o.dma:
                s = dma_slot_ctr[E] % self.n_dma_sems
                dma_slot_ctr[E] += 1
                o.slot = s
                pj = slot_last.get((E, s))
                if pj is not None and not kn.get(("d", pj)):
                    o.prevwait = pj
                    kn = dict(kn)
                    kn[("d", pj)] = 1
                slot_last[(E, s)] = i
            known[E] = kn
            snap[i] = kn
        csem = {e: [] for e in COMPUTE}
        ccount = {e: 0 for e in COMPUTE}
        dsem = {}
        dcount = {}
        nw = 0
        for i, o in enumerate(ops):
            E = o.eng
            e = self.eng[E]
            wl = list(o.waits)
            if o.prevwait is not None:
                wl.append(o.prevwait)
            for j in wl[1:]:
                p = ops[j]
                e.wait_ge(p.sem, p.sigval)
                nw += 1
            ins = o.fn(e)
            if wl:
                p = ops[wl[0]]
                ins._wait_ge(p.sem, p.sigval)
                nw += 1
            if o.dma:
                key = (E, o.slot)
                if key not in dsem:
                    dsem[key] = nc.alloc_semaphore("d_%s_%d" % key)
                    dcount[key] = 0
                dcount[key] += 16
                o.sem = dsem[key]
                o.sigval = dcount[key]
                ins.then_inc(o.sem, 16)
            elif o.signal:
                if ccount[E] % EPOCH == 0:
                    csem[E].append(nc.alloc_semaphore("c_%s_%d" % (E, len(csem[E]))))
                ccount[E] += 1
                o.sem = csem[E][-1]
                o.sigval = (ccount[E] - 1) % EPOCH + 1
                ins.then_inc(o.sem, 1)
        sp = self.eng["sp"]
        for key, s in dsem.items():
            sp.wait_ge(s, dcount[key])
        self.stats = dict(n_ops=len(ops), n_waits=nw,
                          n_sig=sum(ccount.values()), per_eng=dict(seqc))
        return self.stats
S = 2048
D = 1024
NT = 16
ALPHA = (2 * 4) ** 0.25
LN_EPS = 1e-5
NEG = -30000.0
SLOT = 2176
NWS = 3
NAR = 17


class Ctx:
    pass


def build_program(layers=("A0", "P0", "A1", "P1"), nseq=2, dbg=None):
    nc = bass.Bass("TRN2", target_bir_lowering=False)
    P = Prog(nc)
    c = Ctx()
    c.nc, c.P = nc, P

    def din(name, shape):
        return nc.dram_tensor(name, list(shape), F32, kind="ExternalInput").ap()

    c.x = din("x", [2, S, D])
    c.aw_in = din("aw_in", [2, 8, 128, 8, 512])
    c.aw_ff = din("aw_ff", [2, 128, 8, 8])
    c.aw_out = din("aw_out", [2, 128, 8, 1024])
    c.ab_f = din("ab_f", [8, 2])
    c.pw_u = din("pw_u", [2, 4, 128, 8, 512])
    c.pw_g = din("pw_g", [2, 4, 128, 8, 512])
    c.pw_grp = din("pw_grp", [2, 4, 128, 4, 512])
    c.pw_out = din("pw_out", [2, 4, 128, 4, 1024])
    c.psc_d = din("psc", [128, 2, 16])
    c.lng_d = din("lng", [128, 4, 1024])
    c.lnb_d = din("lnb", [128, 4, 1024])
    c.cident_d = din("c_ident", [128, 128])
    c.ccausal_d = din("c_causal", [128, 128])
    c.cband_d = din("c_band", [128, 4, 144])
    c.cfirst_d = din("c_first", [128, 4, 2, 128])
    c.cblk_d = din("c_blk", [8, 2048])
    c.cpast_d = din("c_past", [128, 16, 8])
    c.cnotown_d = din("c_notown", [128, 16, 8])
    c.out = nc.dram_tensor("out", [2, S, D], F32, kind="ExternalOutput").ap()

    c.X = P.tile("X", [128, NT, D], F32, parts=NT)
    c.XT = P.tile("XT", [128, 8, S], BF16, parts=NT)
    c.WS = [P.tile("WS", [128, 4096], BF16) for _ in range(NWS)]
    c.WSH = P.tile("WSH", [128, 2048], BF16)
    c.AR = [P.tile("AR", [128, SLOT], BF16, parts=(4 if i == 13 else 1)) for i in range(NAR)]
    c.PS = [P.tile("PS", [128, 512], F32, space="psum") for _ in range(8)]
    c.ident = P.tile("ident", [128, 128], BF16)
    c.identF = P.tile("identF", [128, 128], F32)
    c.causal = P.tile("causal", [128, 128], BF16)
    c.band = P.tile("band", [128, 4, 144], BF16)
    c.first = P.tile("first", [128, 4, 2, 128], BF16)
    c.LNG = c.AR[12]
    c.LNB = c.AR[13]
    c.lng_v = c.LNG[:, 0:2048].bitcast(F32)
    c.lnb_v = c.LNB[:, 0:2048].bitcast(F32)
    c.psc = P.tile("psc", [128, 2, 16], F32)
    c.lst = P.tile("lst", [128, NT, 12], F32, parts=NT)
    c.lmv = P.tile("lmv", [128, NT, 4], F32, parts=NT)
    c.dbg = dbg
    c.ws_i = 0
    c.ps_i = 0
    c.sg_i = 0
    c.ev_i = 0

    P.dma("pool", c.ident[:], c.cident_d[:, :], writes=[c.ident])
    P.dma("sp", c.identF[:], c.cident_d[:, :], writes=[c.identF])
    P.dma("pool", c.causal[:], c.ccausal_d[:, :], writes=[c.causal])
    P.dma("pool", c.band[:], c.cband_d[:, :, :], writes=[c.band])
    P.dma("pool", c.first[:], c.cfirst_d[:, :, :, :], writes=[c.first])
    P.dma("sp", c.psc[:], c.psc_d[:, :, :], writes=[c.psc])
    attn_setup(c)

    for seq in range(nseq):
        xv = c.x[seq].rearrange("(t p) d -> p t d", p=128)
        for t in range(NT):
            P.dma("sp", c.X[:, t, :], xv[:, t, :], writes=[(c.X, t)])
        L = [make_layer(c, l) for l in layers]
        L[0].begin()
        skew([lambda t: xt_transpose(c, t), lambda t: xt_evac(c, t), L[0].hook])
        for li, ly in enumerate(L):
            last = (li == len(L) - 1)
            ly.main()
            nxt = None if last else L[li + 1]
            layer_norm(c, ly.gl, nxt, seq)
    st = P.finalize()
    return nc, st


def make_layer(c, l):
    j = int(l[1])
    return AttnLayer(c, j) if l[0] == "A" else PoolLayer(c, j)


def dbg_tap(c, name, tile, ap, shape, dtype):
    d = c.nc.dram_tensor(name, list(shape), dtype, kind="ExternalOutput").ap()
    c.P.dma("sp", d, ap, reads=[tile], writes=["dbg_" + name])


def next_ps(c):
    t = c.PS[c.ps_i % 6]
    c.ps_i += 1
    return t


def next_ws(c):
    t = c.WS[c.ws_i % NWS]
    c.ws_i += 1
    return t


def evac(c, out_ap, in_ap, reads, writes, scale=None, eng=None):
    c.ev_i += 1
    if eng is None:
        eng = "act" if c.ev_i % 2 == 0 else "dve"
    if eng == "act":
        if scale is None:
            c.P.op("act", "activation", out=out_ap, in_=in_ap, func=AF.Copy, reads=reads, writes=writes)
        else:
            c.P.op("act", "activation", out=out_ap, in_=in_ap, func=AF.Copy, scale=scale,
                   reads=reads, writes=writes)
    else:
        if scale is None:
            c.P.op("dve", "tensor_copy", out_ap, in_ap, reads=reads, writes=writes)
        else:
            c.P.op("dve", "tensor_scalar_mul", out_ap, in_ap, scale, reads=reads, writes=writes)


def load_w(c, view, src, tile):
    c.P.dma("pool", view, src, writes=[tile])


def xt_transpose(c, t):
    P = c.P
    banks = [c.PS[6], c.PS[7]]
    for k in range(8):
        ps = banks[k // 4]
        P.op("pe", "transpose", ps[:, (k % 4) * 128:(k % 4 + 1) * 128],
             c.X[:, t, k * 128:(k + 1) * 128], c.identF[:],
             reads=[(c.X, t), c.identF], writes=[ps])


def xt_evac(c, t):
    banks = [c.PS[6], c.PS[7]]
    for hf in range(2):
        evac(c, c.XT[:, 4 * hf:4 * hf + 4, t * 128:(t + 1) * 128],
             banks[hf][:].rearrange("p (k q) -> p k q", q=128), [banks[hf]], [(c.XT, t)],
             eng="act")


def skew(stages, n=NT):
    ns = len(stages)
    for i in range(n + ns - 1):
        for sidx in range(ns - 1, -1, -1):
            t = i - sidx
            if 0 <= t < n:
                stages[sidx](t)


def ln_stats(c, t, bank):
    P = c.P
    P.op("act", "activation", out=c.X[:, t, :], in_=c.X[:, t, :], func=AF.Identity,
         accum_out=c.lst[:, t, 0:1], reads=[(c.X, t)], writes=[(c.X, t), (c.lst, t)])
    ps = bank(c)
    for hf in range(2):
        P.op("act", "activation", out=ps[:], in_=c.X[:, t, hf * 512:(hf + 1) * 512], func=AF.Square,
             accum_out=c.lst[:, t, 1 + hf:2 + hf], reads=[(c.X, t)], writes=[ps, (c.lst, t)])


def layer_norm(c, gl, nxt, seq=0):
    P = c.P
    P.dma("sp", c.lng_v, c.lng_d[:, gl, :], writes=[c.LNG])
    P.dma("sp", c.lnb_v, c.lnb_d[:, gl, :], writes=[c.LNB])
    if nxt is not None:
        nxt.begin()
    P.op("dve", "tensor_scalar_mul", c.lmv[:, :, 0:1], c.lst[:, :, 0:1], 1.0 / D,
         reads=[c.lst], writes=[c.lmv])
    P.op("dve", "tensor_tensor", c.lst[:, :, 3:4], c.lst[:, :, 1:2], c.lst[:, :, 2:3], ALU.add,
         reads=[c.lst], writes=[c.lst])
    P.op("dve", "tensor_tensor", c.lst[:, :, 4:5], c.lmv[:, :, 0:1], c.lmv[:, :, 0:1], ALU.mult,
         reads=[c.lmv], writes=[c.lst])
    P.op("dve", "scalar_tensor_tensor", c.lmv[:, :, 1:2], c.lst[:, :, 3:4], 1.0 / D,
         c.lst[:, :, 4:5], ALU.mult, ALU.subtract, reads=[c.lst], writes=[c.lmv])
    P.op("dve", "tensor_scalar_add", c.lmv[:, :, 2:3], c.lmv[:, :, 1:2], LN_EPS,
         reads=[c.lmv], writes=[c.lmv])
    P.op("act", "activation", out=c.lmv[:, :, 3:4], in_=c.lmv[:, :, 2:3], func=AF.Sqrt,
         reads=[c.lmv], writes=[c.lmv])
    P.op("dve", "reciprocal", c.lmv[:, :, 2:3], c.lmv[:, :, 3:4], reads=[c.lmv], writes=[c.lmv])
    P.op("dve", "scalar_tensor_tensor", c.lmv[:, :, 3:4], c.lmv[:, :, 0:1], -1.0,
         c.lmv[:, :, 2:3], ALU.mult, ALU.mult, reads=[c.lmv], writes=[c.lmv])
    def sA(t):
        P.op("act", "activation", out=c.X[:, t, :], in_=c.X[:, t, :], func=AF.Identity,
             scale=c.lmv[:, t, 2:3], bias=c.lmv[:, t, 3:4],
             reads=[(c.X, t), (c.lmv, t)], writes=[(c.X, t)])

    def sB(t):
        P.op("dve", "tensor_tensor", c.X[:, t, :], c.X[:, t, :], c.lng_v, ALU.mult,
             reads=[(c.X, t), c.LNG], writes=[(c.X, t)])

    def sC(t):
        P.op("dve", "tensor_tensor", c.X[:, t, :], c.X[:, t, :], c.lnb_v, ALU.add,
             reads=[(c.X, t), c.LNB], writes=[(c.X, t)])

    def sOut(t):
        ov = c.out[seq].rearrange("(t p) d -> p t d", p=128)
        P.dma("sp", ov[:, t, :], c.X[:, t, :], reads=[(c.X, t)], writes=["out%d_%d" % (seq, t)])

    stages = [sA, sB, sC]
    if nxt is not None:
        stages += [lambda t: xt_transpose(c, t), lambda t: xt_evac(c, t), nxt.hook]
    else:
        stages += [sOut]
    skew(stages)


class PoolLayer:
    def __init__(self, c, j):
        self.c, self.j = c, j
        self.gl = 2 * j + 1
        self.st = {}

    def load(self, g, which):
        c, j = self.c, self.j
        st = self.st.setdefault(g, dict(g=g))
        if which == "wu":
            t = c.WS[0]
            st["wu_t"], st["wu"] = t, t[:].rearrange("p (k n) -> p k n", n=512)
            load_w(c, st["wu"], c.pw_u[j, g], t)
        elif which == "wg":
            t = c.WS[1]
            st["wg_t"], st["wg"] = t, t[:].rearrange("p (k n) -> p k n", n=512)
            load_w(c, st["wg"], c.pw_g[j, g], t)
        elif which == "wgrp":
            t = c.WSH
            st["wgrp_t"], st["wgrp"] = t, t[:].rearrange("p (k n) -> p k n", n=512)
            load_w(c, st["wgrp"], c.pw_grp[j, g], t)
        else:
            t = c.WS[2]
            st["wout_t"], st["wout"] = t, t[:].rearrange("p (k n) -> p k n", n=1024)
            load_w(c, st["wout"], c.pw_out[j, g], t)

    def begin(self):
        for w in ("wu", "wg", "wgrp", "wout"):
            self.load(0, w)

    def hook(self, t):
        self.u_tile(0, t)
        if t % 4 == 3:
            self.gate_chunk(0, t // 4)

    def u_tile(self, g, t):
        c, P, st = self.c, self.c.P, self.st[g]
        UT = c.AR[0:4]
        ps = next_ps(c)
        for k in range(8):
            P.op("pe", "matmul", ps[:], c.XT[:, k, t * 128:(t + 1) * 128], st["wu"][:, k, :],
                 start=(k == 0), stop=(k == 7), reads=[(c.XT, t), st["wu_t"]], writes=[ps])
        ut = UT[t // 4]
        evac(c, ut[:, (t % 4) * 512:(t % 4 + 1) * 512], ps[:], [ps], [ut])

    def gate_chunk(self, g, q):
        c, P, st = self.c, self.c.P, self.st[g]
        ZT = c.AR[8:12]
        for et in range(4):
            psg = next_ps(c)
            for k in range(8):
                P.op("pe", "matmul", psg[:], st["wg"][:, k, et * 128:(et + 1) * 128],
                     c.XT[:, k, q * 512:(q + 1) * 512], start=(k == 0), stop=(k == 7),
                     reads=[st["wg_t"], (c.XT, range(4 * q, 4 * q + 4))], writes=[psg])
            P.op("act", "activation", out=ZT[et][:, q * 512:(q + 1) * 512], in_=psg[:], func=AF.Silu,
                 reads=[psg], writes=[ZT[et]])

    def rest(self, g):
        c, P, st, j = self.c, self.c.P, self.st[g], self.j
        UT, PT, ZT = c.AR[0:4], c.AR[4:8], c.AR[8:12]
        wgrp, wout = st["wgrp"], st["wout"]
        for ct in range(4):
            for q in range(4):
                ps = next_ps(c)
                for i in range(4):
                    tt = 4 * q + i
                    ut = UT[tt // 4]
                    lhs = ut[:, (tt % 4) * 512 + ct * 128:(tt % 4) * 512 + (ct + 1) * 128]
                    o = ps[:, i * 128:(i + 1) * 128]
                    if tt == 0:
                        P.op("pe", "matmul", o, lhs, c.first[:, g, 0, :], start=True, stop=False,
                             reads=[ut, c.first], writes=[ps])
                        P.op("pe", "matmul", o, lhs, c.first[:, g, 1, :], start=False, stop=True,
                             reads=[ut, c.first], writes=[ps])
                    else:
                        P.op("pe", "matmul", o, lhs, c.band[:, g, 0:128], start=True, stop=False,
                             reads=[ut, c.band], writes=[ps])
                        up = UT[(tt - 1) // 4]
                        lhp = up[:, ((tt - 1) % 4) * 512 + ct * 128:((tt - 1) % 4) * 512 + (ct + 1) * 128]
                        P.op("pe", "matmul", o[:, 0:16], lhp, c.band[:, g, 128:144],
                             start=False, stop=True, reads=[up, c.band], writes=[ps])
                evac(c, PT[ct][:, q * 512:(q + 1) * 512], ps[:], [ps], [PT[ct]])
        if g + 1 < 4:
            self.load(g + 1, "wu")
            self.load(g + 1, "wg")
        for et in range(4):
            ge = 4 * g + et
            for q in range(4):
                psy = next_ps(c)
                for ct in range(4):
                    P.op("pe", "matmul", psy[:], wgrp[:, ct, et * 128:(et + 1) * 128],
                         PT[ct][:, q * 512:(q + 1) * 512], start=(ct == 0), stop=(ct == 3),
                         reads=[st["wgrp_t"], PT[ct]], writes=[psy])
                zs = ZT[et][:, q * 512:(q + 1) * 512]
                P.op("dve", "scalar_tensor_tensor", zs, psy[:], c.psc[:, j, ge:ge + 1], zs,
                     ALU.mult, ALU.mult, reads=[psy, ZT[et], c.psc], writes=[ZT[et]])
        if g + 1 < 4:
            self.load(g + 1, "wgrp")
        for t in range(NT):
            for nh in range(2):
                ps = next_ps(c)
                for et in range(4):
                    P.op("pe", "matmul", ps[:], ZT[et][:, t * 128:(t + 1) * 128],
                         wout[:, et, nh * 512:(nh + 1) * 512], start=(et == 0), stop=(et == 3),
                         reads=[ZT[et], st["wout_t"]], writes=[ps])
                xo = c.X[:, t, nh * 512:(nh + 1) * 512]
                if g == 0:
                    P.op("dve", "scalar_tensor_tensor", xo, xo, ALPHA, ps[:], ALU.mult, ALU.add,
                         reads=[(c.X, t), ps], writes=[(c.X, t)])
                else:
                    P.op("dve", "tensor_tensor", xo, xo, ps[:], ALU.add,
                         reads=[(c.X, t), ps], writes=[(c.X, t)])
            if g == 3:
                ln_stats(c, t, next_ps)
        if g + 1 < 4:
            self.load(g + 1, "wout")

    def main(self):
        for g in range(4):
            if g > 0:
                for t in range(NT):
                    self.u_tile(g, t)
                    if t % 4 == 3:
                        self.gate_chunk(g, t // 4)
            self.rest(g)


def attn_setup(c):
    P = c.P
    c.past = P.tile("past", [128, 128], F32)
    c.notown = P.tile("notown", [128, 128], F32)
    c.abf = P.tile("abf", [8, 2], F32)
    c.wff = P.tile("wff", [128, 8, 8], BF16)
    c.gm = [P.tile("gm", [128, 128], F32) for _ in range(2)]
    c.mx = [P.tile("mx", [128, 128], F32) for _ in range(2)]
    c.sel = [P.tile("sel", [128, 128], F32) for _ in range(2)]
    c.mb = [P.tile("mb", [128, 128], BF16) for _ in range(2)]
    c.mm = [P.tile("mm", [128, 3, 16], F32) for _ in range(2)]
    c.km = P.tile("km", [128, 8], F32)
    c.kmb = [P.tile("kmb", [64, 8], BF16) for _ in range(2)]
    c.rden = [P.tile("rden", [128, 4], F32) for _ in range(2)]
    P.dma("sp", c.past[:], c.cpast_d.rearrange("p a b -> p (a b)"), writes=[c.past])
    P.dma("sp", c.notown[:], c.cnotown_d.rearrange("p a b -> p (a b)"), writes=[c.notown])
    P.dma("sp", c.abf[:], c.ab_f[:, :], writes=[c.abf])
    c.fox_al = [c.AR[i].alias() for i in (4, 5, 6, 7)]
    c.pa_i = 0
    c.psS_i = 0
    c.po_i = 0
    c.pt_i = 0
    c.rd_i = 0


def ps_a(c):
    t = c.PS[c.pa_i % 3]
    c.pa_i += 1
    return t


def ps_s(c):
    t = c.PS[3 + c.psS_i % 3]
    c.psS_i += 1
    return t


def ps_o(c):
    t = c.PS[6 + c.po_i % 2]
    c.po_i += 1
    return t


def vp_sgt(c, u):
    VP_t, SGT_t = (c.AR[8], c.AR[9]) if u % 2 == 0 else (c.AR[15], c.AR[16])
    VP = VP_t[:, 0:2080].rearrange("p (t h d) -> p t h d", h=2, d=65)
    SGT = SGT_t[:, 0:2048].rearrange("p (t d) -> p t d", d=128)
    return VP_t, SGT_t, VP, SGT


class AttnLayer:
    def __init__(self, c, j):
        self.c, self.j = c, j
        self.gl = 2 * j
        self.pairs = {}
        self.fq, self.fgen = 0, None

    def fox_gen(self, q):
        c, P, j = self.c, self.c.P, self.j
        CC = c.AR[14]
        tA, tM, tC, tR = c.fox_al
        R0 = slice(96, 104)
        A = tA[R0, 0:1024].bitcast(F32)
        M = tM[R0, 0:1024].bitcast(F32)
        Cc = tC[R0, 0:1024].bitcast(F32)
        R = tR[R0, 0:1024].bitcast(F32)
        fcar = tM[R0, 2048:2050].bitcast(F32)
        c1t = tA[R0, 1024:1536]
        T2 = tA[R0, 1536:2048]
        bf = c.abf[:, j:j + 1]
        if q == 0:
            P.dma("pool", c.wff[:], c.aw_ff[j], writes=[c.wff])
        ps = ps_a(c)
        for k in range(8):
            P.op("pe", "matmul", ps[0:8, :], c.wff[:, k, :], c.XT[:, k, q * 512:(q + 1) * 512],
                 start=(k == 0), stop=(k == 7),
                 reads=[c.wff, (c.XT, range(4 * q, 4 * q + 4))], writes=[ps])
        P.op("act", "activation", out=A, in_=ps[0:8, :], func=AF.Abs, bias=bf,
             reads=[ps, c.abf], writes=[tA])
        P.op("dve", "tensor_scalar", M, ps[0:8, :], bf, 0.0, ALU.add, ALU.min,
             reads=[ps, c.abf], writes=[tM])
        yield
        P.op("act", "activation", out=A, in_=A, func=AF.Exp, scale=-1.0, reads=[tA], writes=[tA])
        yield
        P.op("act", "activation", out=A, in_=A, func=AF.Ln, bias=1.0, reads=[tA], writes=[tA])
        yield
        P.op("dve", "tensor_tensor", M, M, A, ALU.subtract, reads=[tM, tA], writes=[tM])
        yield
        P.op("dve", "tensor_scalar_mul", M, M, 0.5, reads=[tM], writes=[tM])
        yield
        init = 0.0 if q == 0 else fcar
        P.op("dve", "tensor_tensor_scan", Cc, M, M, init, ALU.add, ALU.add,
             reads=[tM], writes=[tC])
        yield
        if q < 3:
            P.op("dve", "tensor_copy", fcar, Cc[:, 511:512], reads=[tC], writes=[tM])
        hs = slice(q * 512, (q + 1) * 512)
        P.op("dve", "tensor_copy", c1t, Cc, reads=[tC], writes=[tA])
        yield
        P.op("dve", "tensor_copy", CC[0:8, hs], c1t, reads=[tA], writes=[CC])
        P.op("dve", "tensor_scalar_mul", CC[32:40, hs], c1t, -1.0, reads=[tA], writes=[CC])
        P.op("dve", "tensor_tensor", R, Cc, c1t, ALU.subtract, reads=[tC, tA], writes=[tR])
        yield
        P.op("dve", "tensor_copy", T2, R, reads=[tR], writes=[tA])
        yield
        P.op("dve", "tensor_scalar_mul", CC[64:72, hs], T2, -1.0, reads=[tA], writes=[CC])
        P.op("dve", "tensor_tensor", R, R, T2, ALU.subtract, reads=[tR, tA], writes=[tR])
        yield
        P.op("dve", "tensor_scalar_mul", CC[96:104, hs], R, -1.0, reads=[tR], writes=[CC])
        yield

    def fox_pull(self, tmax, n):
        while n > 0 and self.fq < 4 and 4 * self.fq + 3 <= tmax:
            if self.fgen is None:
                self.fgen = self.fox_gen(self.fq)
            try:
                next(self.fgen)
                n -= 1
            except StopIteration:
                self.fgen = None
                self.fq += 1

    def pair_begin(self, u):
        c, j = self.c, self.j
        w_t = c.WS[u % 2]
        w = w_t[:].rearrange("p (k n) -> p k n", n=512)
        load_w(c, w, c.aw_in[j, u], w_t)
        base = 4 * (u % 2)
        AR = c.AR
        st = dict(u=u, fox=(u < 4), KA=(68 if u < 4 else 72), w=w, w_t=w_t,
                  QT=[AR[base + 0], AR[base + 1]], KT=[AR[base + 2], AR[base + 3]])
        self.pairs[u] = st
        return st

    def qk_gen(self, u, q, ev="dve"):
        c, P, st = self.c, self.c.P, self.pairs[u]
        w, w_t, QT, KT, fox = st["w"], st["w_t"], st["QT"], st["KT"], st["fox"]
        ps = ps_a(c)
        for k in range(8):
            P.op("pe", "matmul", ps[:], w[:, k, 0:128], c.XT[:, k, q * 512:(q + 1) * 512],
                 start=(k == 0), stop=(k == 7),
                 reads=[w_t, (c.XT, range(4 * q, 4 * q + 4))], writes=[ps])
            if k < 7:
                yield
        for hh in range(2):
            evac(c, QT[hh][0:64, q * 512:(q + 1) * 512], ps[hh * 64:(hh + 1) * 64, :],
                 [ps], [QT[hh]], scale=0.125, eng=ev)
        yield
        ps = ps_a(c)
        for k in range(8):
            P.op("pe", "matmul", ps[:], w[:, k, 128:256], c.XT[:, k, q * 512:(q + 1) * 512],
                 start=(k == 0), stop=(k == 7),
                 reads=[w_t, (c.XT, range(4 * q, 4 * q + 4))], writes=[ps])
            if k < 7:
                yield
        for hh in range(2):
            evac(c, KT[hh][0:64, q * 512:(q + 1) * 512], ps[hh * 64:(hh + 1) * 64, :],
                 [ps], [KT[hh]], eng=(ev if fox else "dve"))
        if not fox:
            P.op("dve", "tensor_reduce", c.km[:, 2 * q:2 * q + 2],
                 ps[:].rearrange("p (a b) -> p a b", b=256), AX.X, ALU.add,
                 reads=[ps], writes=[c.km])
        yield

    def qk_chunk(self, u, q):
        for _ in self.qk_gen(u, q, ev="dve"):
            pass

    def vg_gen(self, u, t4, ev="dve"):
        c, P, st = self.c, self.c.P, self.pairs[u]
        w, w_t = st["w"], st["w_t"]
        VP_t, SGT_t, VP, SGT = vp_sgt(c, u)
        for part in range(2):
            ps = ps_a(c)
            col = 256 if part == 0 else 384
            for ti in range(4):
                t = 4 * t4 + ti
                for k in range(8):
                    P.op("pe", "matmul", ps[:, ti * 128:(ti + 1) * 128],
                         c.XT[:, k, t * 128:(t + 1) * 128], w[:, k, col:col + 128],
                         start=(k == 0), stop=(k == 7), reads=[w_t, (c.XT, t)], writes=[ps])
                    if k % 4 == 3 and not (ti == 3 and k == 7):
                        yield
            if part == 0:
                evac(c, VP[:, 4 * t4:4 * t4 + 4, :, 0:64],
                     ps[:].rearrange("p (t h d) -> p t h d", h=2, d=64), [ps], [VP_t],
                     scale=0.5, eng=ev)
            else:
                sg = SGT[:, 4 * t4:4 * t4 + 4, :]
                g3 = ps[:].rearrange("p (t d) -> p t d", d=128)
                P.op("act", "activation", out=sg, in_=g3, func=AF.Tanh, scale=0.5,
                     reads=[ps], writes=[SGT_t])
                P.op("dve", "scalar_tensor_tensor", sg, sg, 1.0, g3, ALU.add, ALU.mult,
                     reads=[SGT_t, ps], writes=[SGT_t])
            yield

    def vg_group(self, u, t4):
        for _ in self.vg_gen(u, t4, ev="act"):
            pass

    def aug_start(self, u):
        c, P, st = self.c, self.c.P, self.pairs[u]
        QT, KT, fox = st["QT"], st["KT"], st["fox"]
        CC = c.AR[14]
        if fox:
            for hh in range(2):
                hf = 2 * (u % 4) + hh
                P.op("pool", "memset", QT[hh][64:68, 0:2048], 1.0, writes=[QT[hh]])
                P.op("pool", "memset", KT[hh][64:68, 0:2048], 1.0, writes=[KT[hh]])
                P.dma("sp", QT[hh][64:65, 0:2048], CC[hf:hf + 1, 0:2048], reads=[CC], writes=[QT[hh]])
                for r in range(3):
                    P.dma("sp", KT[hh][65 + r:66 + r, 0:2048],
                          CC[32 * (r + 1) + hf:32 * (r + 1) + hf + 1, 0:2048],
                          reads=[CC], writes=[KT[hh]])
        else:
            for hh in range(2):
                P.op("dve", "tensor_scalar_mul", c.kmb[hh][:], c.km[hh * 64:(hh + 1) * 64, :],
                     1.0 / 256.0, reads=[c.km], writes=[c.kmb[hh]])
                P.dma("pool", KT[hh][64:72, 0:2048], c.cblk_d[:, :], writes=[KT[hh]])

    def moba_gate_a_gen(self, u, hh):
        c, P, st = self.c, self.c.P, self.pairs[u]
        QT_t, kmb = st["QT"][hh], c.kmb[hh]
        gm, g2, e, mb, mm = c.gm[hh], c.mx[hh], c.sel[hh], c.mb[hh], c.mm[hh]
        ps = ps_a(c)
        for i in range(16):
            P.op("pe", "matmul", ps[:, i * 8:(i + 1) * 8], QT_t[0:64, i * 128:(i + 1) * 128], kmb[:],
                 start=True, stop=True, reads=[QT_t, kmb], writes=[ps])
            if i % 4 == 3:
                yield
        P.op("dve", "tensor_tensor", gm[:], ps[:, 0:128], c.past[:], ALU.add,
             reads=[ps, c.past], writes=[gm])
        yield

        def v3(t):
            return t[:].rearrange("p (a b) -> p a b", b=8)

        def bc(k):
            return mm[:, k, :].unsqueeze(2).to_broadcast([128, 16, 8])

        KO = -1e32
        P.op("dve", "tensor_reduce", mm[:, 0, :], v3(gm), AX.X, ALU.max, reads=[gm], writes=[mm])
        yield
        P.op("dve", "tensor_tensor", v3(e), v3(gm), bc(0), ALU.is_ge, reads=[gm, mm], writes=[e])
        yield
        P.op("dve", "scalar_tensor_tensor", g2[:], e[:], KO, gm[:], ALU.mult, ALU.add,
             reads=[e, gm], writes=[g2])
        yield
        P.op("dve", "tensor_reduce", mm[:, 1, :], v3(g2), AX.X, ALU.max, reads=[g2], writes=[mm])
        yield
        P.op("dve", "tensor_tensor", v3(e), v3(g2), bc(1), ALU.is_ge, reads=[g2, mm], writes=[e])
        yield
        P.op("dve", "scalar_tensor_tensor", g2[:], e[:], KO, g2[:], ALU.mult, ALU.add,
             reads=[e, g2], writes=[g2])
        yield
        P.op("dve", "tensor_reduce", mm[:, 2, :], v3(g2), AX.X, ALU.max, reads=[g2], writes=[mm])
        yield
        P.op("dve", "tensor_tensor", v3(e), v3(gm), bc(2), ALU.is_ge, reads=[gm, mm], writes=[e])
        yield
        P.op("dve", "scalar_tensor_tensor", mb[:], e[:], -1.0, c.notown[:], ALU.add, ALU.mult,
             reads=[e, c.notown], writes=[mb])
        yield

    def moba_gate_a(self, u, hh):
        for _ in self.moba_gate_a_gen(u, hh):
            pass

    def moba_gate_b(self, u, hh):
        c, P, st = self.c, self.c.P, self.pairs[u]
        QT_t, mb = st["QT"][hh], c.mb[hh]
        MBT_t = c.AR[14]
        MBT = MBT_t[32 * hh:32 * hh + 8, 0:2048]
        for half in range(2):
            pt = ps_a(c)
            ptb = pt[:].bitcast(BF16)
            for ii in range(8):
                i = 8 * half + ii
                P.op("pe", "transpose", ptb[0:8, ii * 128:(ii + 1) * 128], mb[:, i * 8:(i + 1) * 8],
                     c.ident[:], reads=[mb, c.ident], writes=[pt])
            evac(c, MBT[:, half * 1024:(half + 1) * 1024], ptb[0:8, :], [pt], [MBT_t], eng="dve")
        P.dma("sp", QT_t[64:72, 0:2048], MBT, reads=[MBT_t], writes=[QT_t])

    def core_chunk(self, u, hh, qc, filler=None):
        c, P, st = self.c, self.c.P, self.pairs[u]
        QT_t, KT_t, KA = st["QT"][hh], st["KT"][hh], st["KA"]
        YP_t, PTS_t = c.AR[10], c.AR[13]
        VP_t, SGT_t, VP, SGT = vp_sgt(c, u)
        YP = YP_t[:, 0:2048].rearrange("p (t d) -> p t d", d=128)
        po_t = ps_o(c)
        po = po_t[:, 0:260].rearrange("p (i d) -> p i d", d=65)
        nj = 4 * qc + 4
        pts = {}

        def s_tile(jk):
            n0 = max(128 * jk, 512 * qc)
            N = 512 * qc + 512 - n0
            ps = ps_s(c)
            diag = jk >= 4 * qc
            P.op("pe", "matmul", ps[:, 0:N], KT_t[0:KA, jk * 128:(jk + 1) * 128],
                 QT_t[0:KA, n0:n0 + N], start=True, stop=(not diag),
                 reads=[KT_t, QT_t], writes=[ps])
            if diag:
                P.op("pe", "matmul", ps[:, 0:128], c.ident[:], c.causal[:], start=False, stop=True,
                     reads=[c.ident, c.causal], writes=[ps])
            pi = c.pt_i % 4
            c.pt_i += 1
            pt = PTS_t[:, pi * 512:pi * 512 + N]
            P.op("act", "activation", out=pt, in_=ps[:, 0:N], func=AF.Exp,
                 reads=[ps], writes=[(PTS_t, pi)])
            pts[jk] = (pt, pi, n0, N)

        def pv(jk):
            pt, pi, n0, N = pts.pop(jk)
            for ii in range(N // 128):
                qi = n0 // 128 + ii
                P.op("pe", "matmul", po[:, qi % 4, :], pt[:, ii * 128:(ii + 1) * 128],
                     VP[:, jk, hh, :], start=(jk == 0 and ii == 0), stop=(jk == qi),
                     skip_group_check=True,
                     reads=[(PTS_t, pi), VP_t], writes=[po_t])

        LA = 2
        for jk in range(min(LA, nj)):
            s_tile(jk)
        for jk in range(nj):
            if jk + LA < nj:
                s_tile(jk + LA)
            pv(jk)
            if filler is not None:
                next(filler, None)
                next(filler, None)
        rd = c.rden[c.rd_i % 2]
        c.rd_i += 1
        P.op("dve", "reciprocal", rd[:], po[:, :, 64:65], reads=[po_t], writes=[rd])
        yv = YP[:, 4 * qc:4 * qc + 4, hh * 64:(hh + 1) * 64]
        P.op("dve", "tensor_tensor", yv, po[:, :, 0:64],
             rd[:, :].unsqueeze(2).to_broadcast([128, 4, 64]), ALU.mult,
             reads=[po_t, rd], writes=[YP_t])
        P.op("dve", "tensor_tensor", yv, yv, SGT[:, 4 * qc:4 * qc + 4, hh * 64:(hh + 1) * 64], ALU.mult,
             reads=[YP_t, SGT_t], writes=[YP_t])

    def pair_finish(self, u):
        c, P, j = self.c, self.c.P, self.j
        YP_t = c.AR[10]
        YP = YP_t[:, 0:2048].rearrange("p (t d) -> p t d", d=128)
        YGT = c.AR[11:13]
        yg = YGT[u % 2]
        for half in range(2):
            ps = ps_a(c)
            psb = ps[:].bitcast(BF16)
            for ti in range(8):
                t = 8 * half + ti
                P.op("pe", "transpose", psb[:, ti * 128:(ti + 1) * 128], YP[:, t, :], c.ident[:],
                     reads=[YP_t, c.ident], writes=[ps])
            evac(c, yg[:, half * 1024:(half + 1) * 1024], psb, [ps], [yg])
        if u % 2 == 1:
            sgi = u // 2
            wo_t = c.WS[2] if sgi % 2 == 0 else c.WSH
            wo = wo_t[:, 0:2048].rearrange("p (k n) -> p k n", n=1024)
            load_w(c, wo, c.aw_out[j][:, 2 * sgi:2 * sgi + 2, :], wo_t)
            self.wo = (sgi, wo, wo_t)

    def outproj_gen(self):
        c, P = self.c, self.c.P
        sgi, wo, wo_t = self.wo
        YGT = c.AR[11:13]
        for t in range(NT):
            for nh in range(2):
                ps = ps_a(c)
                for pp in range(2):
                    P.op("pe", "matmul", ps[:], YGT[pp][:, t * 128:(t + 1) * 128],
                         wo[:, pp, nh * 512:(nh + 1) * 512], start=(pp == 0), stop=(pp == 1),
                         reads=[YGT[pp], wo_t], writes=[ps])
                xo = c.X[:, t, nh * 512:(nh + 1) * 512]
                if sgi == 0:
                    P.op("dve", "scalar_tensor_tensor", xo, xo, ALPHA, ps[:], ALU.mult, ALU.add,
                         reads=[(c.X, t), ps], writes=[(c.X, t)])
                else:
                    P.op("dve", "tensor_tensor", xo, xo, ps[:], ALU.add,
                         reads=[(c.X, t), ps], writes=[(c.X, t)])
                if not (sgi == 3 and nh == 1):
                    yield
            if sgi == 3:
                ln_stats(c, t, ps_a)
                yield

    def begin(self):
        c, P, j = self.c, self.c.P, self.j
        self.pair_begin(0)
        self.pair_begin(1)
        for par in range(2):
            VP_t, _, VP, _ = vp_sgt(c, par)
            P.op("pool", "memset", VP[:, :, :, 64:65], 1.0, writes=[VP_t])

    def hook(self, t):
        if t % 4 == 3:
            q = t // 4
            self.qk_chunk(0, q)
            self.vg_group(0, q)
        self.fox_pull(t, 3)

    def next_pair_filler(self, u1):
        if u1 < 8:
            for q in range(4):
                yield from self.qk_gen(u1, q)
            self.aug_start(u1)
            self.aug_done = u1
            yield
            if not self.pairs[u1]["fox"]:
                yield from self.moba_gate_a_gen(u1, 0)
                yield from self.moba_gate_a_gen(u1, 1)
                yield
                self.moba_gate_b(u1, 0)
                yield
                self.moba_gate_b(u1, 1)
                self.gate_b_done = u1
                yield
        if u1 % 2 == 1 and u1 >= 3:
            yield from self.outproj_gen()
        if u1 < 8:
            for t4 in range(4):
                yield from self.vg_gen(u1, t4)

    def main(self):
        self.fox_pull(NT - 1, 10 ** 6)
        self.aug_start(0)
        for u in range(8):
            if u + 2 < 8:
                self.pair_begin(u + 2)
            filler = self.next_pair_filler(u + 1)
            self.step_i = 0
            self.heavy = ((u + 1) % 2 == 1 and u + 1 >= 3)
            for hh in range(2):
                for qc in range(4):
                    self.core_chunk(u, hh, qc, filler)
            if filler is not None:
                for _ in filler:
                    pass
            nm = (u + 1 < 8) and not self.pairs[u + 1]["fox"]
            self.pair_finish(u)
            if u == 7:
                for _ in self.outproj_gen():
                    pass


def _consts():
    import ml_dtypes
    bf = ml_dtypes.bfloat16
    cs = {}
    cs["c_ident"] = np.eye(128, dtype=np.float32)
    k = np.arange(128)[:, None]
    q = np.arange(128)[None, :]
    cs["c_causal"] = np.where(k > q, NEG, 0.0).astype(np.float32)
    band = np.zeros((128, 4, 144), np.float32)
    first = np.zeros((128, 4, 2, 128), np.float32)
    s = np.arange(128)[:, None]
    for g, w in enumerate((2, 4, 8, 16)):
        tl = np.arange(144)[None, :]
        m = ((tl - s >= 0) & (tl - s <= w - 1)).astype(np.float32) / w
        m = m - (tl == s).astype(np.float32)
        band[:, g, :] = m
        tl = np.arange(128)[None, :]
        cnt = np.minimum(tl + 1, w).astype(np.float64)
        m = ((tl - s >= 0) & (tl - s <= w - 1)).astype(np.float64) / cnt - (tl == s)
        hi = m.astype(np.float32).astype(bf).astype(np.float32)
        lo = (m - hi).astype(np.float32).astype(bf).astype(np.float32)
        first[:, g, 0, :] = hi
        first[:, g, 1, :] = lo
    cs["c_band"] = band
    cs["c_first"] = first
    kk = np.arange(2048)
    cs["c_blk"] = (kk[None, :] // 256 == np.arange(8)[:, None]).astype(np.float32)
    qt = np.arange(16)[None, :, None]
    n = np.arange(8)[None, None, :]
    past = np.where(n < qt // 2, 0.0, -1e30).astype(np.float32)
    cs["c_past"] = np.broadcast_to(past, (128, 16, 8)).copy()
    cs["c_notown"] = np.broadcast_to((n != qt // 2).astype(np.float32) * (-NEG), (128, 16, 8)).copy()
    return cs


def _kd(w):
    K = w.shape[0] // 128
    return np.ascontiguousarray(w.reshape(K, 128, w.shape[1]).transpose(1, 0, 2))


def prep_weights(attn_w_in, attn_b_f, attn_w_out, pool_w_in, pool_w_grp, pool_scale,
                 pool_w_out, ln_g, ln_b):
    f = np.float32
    W = {}
    aw_in = np.zeros((2, 8, 128, 8, 512), f)
    aw_ff = np.zeros((2, 128, 8, 8), f)
    aw_out = np.zeros((2, 128, 8, 1024), f)
    for j in range(2):
        w = np.asarray(attn_w_in[j], f)
        fq, fk, fv, fg = w[:, 0:512], w[:, 512:1024], w[:, 1024:1536], w[:, 1536:2048]
        ff = w[:, 2048:2056]
        mq, mk, mv, mg = (w[:, 2056:2568], w[:, 2568:3080], w[:, 3080:3592], w[:, 3592:4104])
        for u in range(8):
            if u < 4:
                q_, k_, v_, g_ = fq, fk, fv, fg
                p = u
            else:
                q_, k_, v_, g_ = mq, mk, mv, mg
                p = u - 4
            sl = slice(p * 128, (p + 1) * 128)
            cat = np.concatenate([q_[:, sl], k_[:, sl], v_[:, sl], g_[:, sl]], axis=1)
            aw_in[j, u] = _kd(cat)
        aw_ff[j] = _kd(ff)
        aw_out[j] = _kd(np.asarray(attn_w_out[j], f))
    W["aw_in"], W["aw_ff"], W["aw_out"] = aw_in, aw_ff, aw_out
    W["ab_f"] = np.ascontiguousarray(np.asarray(attn_b_f, f).T)
    pw_u = np.zeros((2, 4, 128, 8, 512), f)
    pw_g = np.zeros((2, 4, 128, 8, 512), f)
    pw_grp = np.zeros((2, 4, 128, 4, 512), f)
    pw_out = np.zeros((2, 4, 128, 4, 1024), f)
    for j in range(2):
        w = np.asarray(pool_w_in[j], f)
        for g in range(4):
            pw_u[j, g] = _kd(w[:, g * 512:(g + 1) * 512])
            pw_g[j, g] = _kd(w[:, 2048 + g * 512:2048 + (g + 1) * 512])
            pw_grp[j, g] = _kd(np.asarray(pool_w_grp[j, g], f))
            pw_out[j, g] = _kd(np.asarray(pool_w_out[j], f)[g * 512:(g + 1) * 512])
    W["pw_u"], W["pw_g"], W["pw_grp"], W["pw_out"] = pw_u, pw_g, pw_grp, pw_out
    W["psc"] = np.ascontiguousarray(np.asarray(pool_scale, f).reshape(2, 16, 128).transpose(2, 0, 1))
    W["lng"] = np.ascontiguousarray(np.broadcast_to(np.asarray(ln_g, f)[None], (128, 4, 1024)))
    W["lnb"] = np.ascontiguousarray(np.broadcast_to(np.asarray(ln_b, f)[None], (128, 4, 1024)))
    W.update(_consts())
    return W


_CACHE = {}


def run(x, W, layers=("A0", "P0", "A1", "P1"), nseq=2, ncores=8, trace=False):
    key = (tuple(layers), nseq)
    if key not in _CACHE:
        _CACHE[key] = build_program(layers, nseq)
    nc, st = _CACHE[key]
    in_maps = []
    for i in range(ncores):
        m = dict(W)
        m["x"] = np.ascontiguousarray(x[2 * i:2 * i + 2])
        in_maps.append(m)
    res = run_bass_kernel_spmd(nc, in_maps, core_ids=list(range(ncores)), trace=trace)
    out = np.concatenate([r["out"] for r in res.results], axis=0)
    return out, res


def kernel(x, attn_w_in, attn_b_f, attn_w_out, pool_w_in, pool_w_grp, pool_scale,
           pool_w_out, ln_g, ln_b):
    W = prep_weights(attn_w_in, attn_b_f, attn_w_out, pool_w_in, pool_w_grp, pool_scale,
                     pool_w_out, ln_g, ln_b)
    out, _ = run(np.asarray(x, np.float32), W)
    return out.astype(np.float32)
```
